# Optimizing a Trainium2 kernel written in Bass

```python
import math
import jax
import jax.numpy as jnp
from jax import lax
import numpy as np

D_MODEL = 1024
BATCH = 32
SEQ = 2048
DEPTH = 2

F32 = jnp.float32
PLE_DIM = 256
EPS = 1e-6
NEG = -1e30
TINY = 1e-30

SSD_W = D_MODEL
SSD_HEAD_DIM = 64
SSD_H = SSD_W // SSD_HEAD_DIM
SSD_G = 2
SSD_HG = SSD_H // SSD_G
SSD_N = 128
SSD_CONV = 5
SSD_XBC = SSD_W + 2 * SSD_G * SSD_N
SSD_CHUNK = 64

HG_W = D_MODEL
HG_HEAD_DIM = 128
HG_H = HG_W // HG_HEAD_DIM
HG_CHUNK = 16

ML_W = D_MODEL
ML_HEAD_DIM = 128
ML_H = ML_W // ML_HEAD_DIM
ML_CHUNK = 64

N_BRANCH = 3
D_FF = 2816
FFN_CONV = 3

IN_SIZES = (SSD_W, SSD_XBC, 2 * SSD_H,
            HG_W, 2 * HG_W, HG_W, HG_W,
            ML_W, ML_W, ML_W, 2 * ML_H, 2 * ML_H, ML_W,
            N_BRANCH * D_MODEL)
N_IN = sum(IN_SIZES)

kernel_name = 'bidir_hybrid_ssd_hgrn2_mlstm_block'


def _rmsnorm(x, g):
    xf = x.astype(F32)
    y = xf * lax.rsqrt(jnp.mean(xf * xf, axis=-1, keepdims=True) + EPS)
    return (y * g.astype(F32)).astype(x.dtype)


def _group_rmsnorm(x, g, n_groups):
    shp = x.shape
    xf = x.astype(F32).reshape(shp[:-1] + (n_groups, shp[-1] // n_groups))
    xf = xf * lax.rsqrt(jnp.mean(xf * xf, axis=-1, keepdims=True) + EPS)
    return (xf.reshape(shp) * g.astype(F32)).astype(x.dtype)


def _head_layernorm(x, g, n_heads):
    shp = x.shape
    xf = x.astype(F32).reshape(shp[:-1] + (n_heads, shp[-1] // n_heads))
    xf = xf - jnp.mean(xf, axis=-1, keepdims=True)
    xf = xf * lax.rsqrt(jnp.mean(xf * xf, axis=-1, keepdims=True) + EPS)
    return (xf.reshape(shp) * g.astype(F32)).astype(x.dtype)


def _dwconv(x, w, b):
    width = w.shape[0]
    pad = width // 2
    n_s = x.shape[1]
    xp = jnp.pad(x, ((0, 0), (pad, pad), (0, 0)))
    y = b
    for j in range(width):
        y = y + xp[:, j:j + n_s, :] * w[j]
    return y


def _split(a, sizes):
    idx, acc = [], 0
    for n in sizes[:-1]:
        acc += n
        idx.append(acc)
    return jnp.split(a, idx, axis=-1)


def _flip(a):
    return jnp.flip(a, axis=1)


def _chunk(a, size):
    n_b, n_s = a.shape[:2]
    return a.reshape((n_b, n_s // size, size) + a.shape[2:]).swapaxes(0, 1)


def _unchunk(a):
    n_c, n_b, size = a.shape[:3]
    return a.swapaxes(0, 1).reshape((n_b, n_c * size) + a.shape[3:])


def _ssd_scan(x, dt, A, bm, cm):
    x, dt, bm, cm = x.astype(F32), dt.astype(F32), bm.astype(F32), cm.astype(F32)
    n_b, _, n_g, n_hg, n_p = x.shape
    log_a = dt * A.astype(F32)
    xdt = x * dt[..., None]
    mask = jnp.tril(jnp.ones((SSD_CHUNK, SSD_CHUNK), dtype=bool))

    def body(state, inp):
        xc, ac, bc, cc = inp
        acum = jnp.cumsum(ac, axis=1)
        act = jnp.moveaxis(acum, 1, -1)
        seg = jnp.exp(jnp.where(mask, act[..., :, None] - act[..., None, :], NEG))
        cb = jnp.einsum('blgn,bsgn->bgls', cc, bc)
        y = jnp.einsum('bgls,bghls,bsghp->blghp', cb, seg, xc)
        y = y + jnp.einsum('blgn,bghpn->blghp', cc, state) * jnp.exp(acum)[..., None]
        to_end = jnp.exp(acum[:, -1:] - acum)
        state = (state * jnp.exp(acum[:, -1])[..., None, None]
                 + jnp.einsum('bsgn,bsgh,bsghp->bghpn', bc, to_end, xc))
        return state, y

    state0 = jnp.zeros((n_b, n_g, n_hg, n_p, bm.shape[-1]), F32)
    _, ys = lax.scan(body, state0, (_chunk(xdt, SSD_CHUNK), _chunk(log_a, SSD_CHUNK),
                                    _chunk(bm, SSD_CHUNK), _chunk(cm, SSD_CHUNK)))
    return _unchunk(ys)


def _hgrn2_scan(q, k, v, log_f):
    n_b, _, n_h, n_k = q.shape
    n_v = v.shape[-1]
    mask = jnp.tril(jnp.ones((HG_CHUNK, HG_CHUNK), dtype=bool))[None, :, :, None, None]

    def body(state, inp):
        qc, kc, vc, fc = inp
        b = jnp.cumsum(fc, axis=1)
        o = jnp.einsum('blhk,bhkv->blhv', qc * jnp.exp(b), state)
        w = jnp.exp(jnp.where(mask, b[:, :, None] - b[:, None, :], NEG))
        att = jnp.einsum('bthk,btshk,bshk->bhts', qc, w, kc)
        o = o + jnp.einsum('bhts,bshv->bthv', att, vc)
        b_end = b[:, -1]
        state = (state * jnp.exp(b_end)[..., None]
                 + jnp.einsum('bshk,bshv->bhkv', kc * jnp.exp(b_end[:, None] - b), vc))
        return state, o

    state0 = jnp.zeros((n_b, n_h, n_k, n_v), F32)
    _, ys = lax.scan(body, state0, (_chunk(q, HG_CHUNK), _chunk(k, HG_CHUNK),
                                    _chunk(v, HG_CHUNK), _chunk(log_f, HG_CHUNK)))
    return _unchunk(ys)


def _mlstm_scan(q, k, v, ig, log_f):
    n_b, _, n_h, n_d = q.shape
    mask = jnp.tril(jnp.ones((ML_CHUNK, ML_CHUNK), dtype=bool))

    def body(carry, inp):
        c, n, m = carry
        qc, kc, vc, ic, fc = inp
        bt = jnp.moveaxis(jnp.cumsum(fc, axis=1), 1, -1)
        it = jnp.moveaxis(ic, 1, -1)
        dmat = jnp.where(mask, bt[..., :, None] - bt[..., None, :] + it[..., None, :], NEG)
        inter = bt + m[..., None]
        m_t = jnp.maximum(inter, jnp.max(dmat, axis=-1))
        sc = jnp.einsum('bthd,bshd->bhts', qc, kc) * jnp.exp(dmat - m_t[..., None])
        w_inter = jnp.exp(inter - m_t)
        num = (jnp.einsum('bhts,bshd->bthd', sc, vc)
               + jnp.einsum('bht,bthk,bhvk->bthv', w_inter, qc, c))
        den = jnp.sum(sc, axis=-1) + w_inter * jnp.einsum('bthk,bhk->bht', qc, n)
        denom = jnp.maximum(jnp.abs(den), jnp.exp(-m_t))
        h = num / jnp.moveaxis(denom, -1, 1)[..., None]
        m_new = m_t[..., -1]
        decay = jnp.exp(bt[..., -1] + m - m_new)
        wk = jnp.exp(bt[..., -1:] - bt + it - m_new[..., None])
        c = c * decay[..., None, None] + jnp.einsum('bhs,bshv,bshk->bhvk', wk, vc, kc)
        n = n * decay[..., None] + jnp.einsum('bhs,bshk->bhk', wk, kc)
        return (c, n, m_new), h

    carry0 = (jnp.zeros((n_b, n_h, n_d, n_d), F32), jnp.zeros((n_b, n_h, n_d), F32),
              jnp.full((n_b, n_h), NEG, F32))
    _, ys = lax.scan(body, carry0, (_chunk(q, ML_CHUNK), _chunk(k, ML_CHUNK), _chunk(v, ML_CHUNK),
                                    _chunk(ig, ML_CHUNK), _chunk(log_f, ML_CHUNK)))
    return _unchunk(ys)


def _ssd_branch(z, xbc, dt_raw, conv_w, conv_b, dt_bias, a_log, d_skip, norm_g):
    n_b, n_s, _ = z.shape
    xbc = jax.nn.silu(_dwconv(xbc, conv_w, conv_b))
    xs, bm, cm = _split(xbc, (SSD_W, SSD_G * SSD_N, SSD_G * SSD_N))
    xs = xs.reshape(n_b, n_s, SSD_G, SSD_HG, SSD_HEAD_DIM)
    bm = bm.reshape(n_b, n_s, SSD_G, SSD_N)
    cm = cm.reshape(n_b, n_s, SSD_G, SSD_N)
    dt = jax.nn.softplus(dt_raw.astype(F32).reshape(n_b, n_s, 2, SSD_G, SSD_HG)
                         + dt_bias.astype(F32).reshape(2, SSD_G, SSD_HG))
    A = -jnp.exp(a_log.astype(F32)).reshape(2, SSD_G, SSD_HG)
    y_f = _ssd_scan(xs, dt[:, :, 0], A[0], bm, cm)
    y_b = _flip(_ssd_scan(_flip(xs), _flip(dt[:, :, 1]), A[1], _flip(bm), _flip(cm)))
    y = y_f + y_b + xs.astype(F32) * d_skip.astype(F32).reshape(SSD_G, SSD_HG, 1)
    y = y.reshape(n_b, n_s, SSD_W).astype(z.dtype)
    return _group_rmsnorm(y * jax.nn.silu(z), norm_g, SSD_G)


def _hgrn2_branch(q_raw, f_raw, i_in, g_out, lb, norm_g):
    n_b, n_s, _ = q_raw.shape
    hshape = (n_b, n_s, HG_H, HG_HEAD_DIM)
    q = jax.nn.silu(q_raw.astype(F32)).reshape(hshape)
    v = i_in.astype(F32).reshape(hshape)
    f_raw = f_raw.astype(F32).reshape(n_b, n_s, 2, HG_W)
    lb = lb.astype(F32)
    f = lb + (1.0 - lb) * jax.nn.sigmoid(f_raw)
    log_f = jnp.log(jnp.maximum(f, TINY))
    k = (1.0 - lb) * jax.nn.sigmoid(-f_raw)
    log_f = log_f.reshape(n_b, n_s, 2, HG_H, HG_HEAD_DIM)
    k = k.reshape(n_b, n_s, 2, HG_H, HG_HEAD_DIM)
    o_f = _hgrn2_scan(q, k[:, :, 0], v, log_f[:, :, 0])
    o_b = _flip(_hgrn2_scan(_flip(q), _flip(k[:, :, 1]), _flip(v), _flip(log_f[:, :, 1])))
    o = _group_rmsnorm((o_f + o_b).reshape(n_b, n_s, HG_W), norm_g, HG_H)
    return (o * jax.nn.silu(g_out.astype(F32))).astype(q_raw.dtype)


def _mlstm_branch(q, k, v, ig_raw, fg_raw, o_raw, i_bias, f_bias, norm_g):
    n_b, n_s, _ = q.shape
    hshape = (n_b, n_s, ML_H, ML_HEAD_DIM)
    qf = q.astype(F32).reshape(hshape)
    kf = k.astype(F32).reshape(hshape) * (ML_HEAD_DIM ** -0.5)
    vf = v.astype(F32).reshape(hshape)
    ig = ig_raw.astype(F32).reshape(n_b, n_s, 2, ML_H) + i_bias.astype(F32)
    log_f = jax.nn.log_sigmoid(fg_raw.astype(F32).reshape(n_b, n_s, 2, ML_H) + f_bias.astype(F32))
    h_f = _mlstm_scan(qf, kf, vf, ig[:, :, 0], log_f[:, :, 0])
    h_b = _flip(_mlstm_scan(_flip(qf), _flip(kf), _flip(vf), _flip(ig[:, :, 1]), _flip(log_f[:, :, 1])))
    hn = _head_layernorm((h_f + h_b).reshape(n_b, n_s, ML_W), norm_g, ML_H)
    return (hn * jax.nn.sigmoid(o_raw.astype(F32))).astype(q.dtype)


def setup_inputs(seed: int = 0) -> dict:
    key = jax.random.key(seed)
    ks = iter(jax.random.split(key, 40))

    def nrm(shape, scale):
        return jax.random.normal(next(ks), shape, F32) * scale

    def gain(shape):
        return 1.0 + nrm(shape, 0.02)

    x = nrm((BATCH, SEQ, D_MODEL), 1.0)
    p = nrm((DEPTH, BATCH, SEQ, PLE_DIM), 1.0)
    norm_mix_g = gain((DEPTH, D_MODEL))
    w_in = nrm((DEPTH, D_MODEL, N_IN), D_MODEL ** -0.5)
    ssd_conv_w = nrm((DEPTH, SSD_CONV, SSD_XBC), SSD_CONV ** -0.5)
    ssd_conv_b = nrm((DEPTH, SSD_XBC), 0.02)
    dt0 = jnp.exp(jax.random.uniform(next(ks), (DEPTH, 2, SSD_H), F32,
                                     minval=math.log(1e-3), maxval=math.log(1e-1)))
    ssd_dt_bias = dt0 + jnp.log(-jnp.expm1(-dt0))
    ssd_a_log = jnp.log(jax.random.uniform(next(ks), (DEPTH, 2, SSD_H), F32, minval=1.0, maxval=16.0))
    ssd_d = gain((DEPTH, SSD_H))
    ssd_norm_g = gain((DEPTH, SSD_W))
    hg_lb_raw = nrm((DEPTH, 2, HG_W), 0.5)
    hg_norm_g = gain((DEPTH, HG_W))
    ml_i_bias = nrm((DEPTH, 2, ML_H), 0.1)
    ml_f_bias = jnp.linspace(3.0, 6.0, ML_H, dtype=F32) + nrm((DEPTH, 2, ML_H), 0.1)
    ml_norm_g = gain((DEPTH, ML_W))
    w_br_ssd = nrm((DEPTH, SSD_W, D_MODEL), SSD_W ** -0.5)
    w_br_hg = nrm((DEPTH, HG_W, D_MODEL), HG_W ** -0.5)
    w_br_ml = nrm((DEPTH, ML_W, D_MODEL), ML_W ** -0.5)
    w_out = nrm((DEPTH, D_MODEL, D_MODEL), D_MODEL ** -0.5)
    norm_ffn_g = gain((DEPTH, D_MODEL))
    w_up = nrm((DEPTH, D_MODEL, 2 * D_FF), D_MODEL ** -0.5)
    ffn_conv_w = nrm((DEPTH, FFN_CONV, 2 * D_FF), FFN_CONV ** -0.5)
    ffn_conv_b = nrm((DEPTH, 2 * D_FF), 0.02)
    w_down = nrm((DEPTH, D_FF, D_MODEL), D_FF ** -0.5)
    w_ple = nrm((DEPTH, PLE_DIM, D_MODEL), PLE_DIM ** -0.5)
    w_ple_gate = nrm((DEPTH, D_MODEL, D_MODEL), D_MODEL ** -0.5)
    final_norm_g = gain((D_MODEL,))
    return {'x': x, 'p': p, 'norm_mix_g': norm_mix_g, 'w_in': w_in,
            'ssd_conv_w': ssd_conv_w, 'ssd_conv_b': ssd_conv_b, 'ssd_dt_bias': ssd_dt_bias,
            'ssd_a_log': ssd_a_log, 'ssd_d': ssd_d, 'ssd_norm_g': ssd_norm_g,
            'hg_lb_raw': hg_lb_raw, 'hg_norm_g': hg_norm_g,
            'ml_i_bias': ml_i_bias, 'ml_f_bias': ml_f_bias, 'ml_norm_g': ml_norm_g,
            'w_br_ssd': w_br_ssd, 'w_br_hg': w_br_hg, 'w_br_ml': w_br_ml, 'w_out': w_out,
            'norm_ffn_g': norm_ffn_g, 'w_up': w_up, 'ffn_conv_w': ffn_conv_w, 'ffn_conv_b': ffn_conv_b,
            'w_down': w_down, 'w_ple': w_ple, 'w_ple_gate': w_ple_gate, 'final_norm_g': final_norm_g}


def reference(x, p, norm_mix_g, w_in, ssd_conv_w, ssd_conv_b, ssd_dt_bias, ssd_a_log, ssd_d,
              ssd_norm_g, hg_lb_raw, hg_norm_g, ml_i_bias, ml_f_bias, ml_norm_g,
              w_br_ssd, w_br_hg, w_br_ml, w_out, norm_ffn_g, w_up, ffn_conv_w, ffn_conv_b,
              w_down, w_ple, w_ple_gate, final_norm_g):
    n_b, n_s, _ = x.shape
    lb_soft = jax.nn.softmax(hg_lb_raw.astype(F32), axis=0)
    hg_lb = jnp.cumsum(lb_soft, axis=0) - lb_soft[0:1]
    h = x
    for l in range(DEPTH):
        u = _rmsnorm(h, norm_mix_g[l])
        (s_z, s_xbc, s_dt, g_q, g_f, g_i, g_g, m_q, m_k, m_v, m_i, m_f, m_o,
         gates) = _split(u @ w_in[l], IN_SIZES)
        y_ssd = _ssd_branch(s_z, s_xbc, s_dt, ssd_conv_w[l], ssd_conv_b[l], ssd_dt_bias[l],
                            ssd_a_log[l], ssd_d[l], ssd_norm_g[l])
        y_hg = _hgrn2_branch(g_q, g_f, g_i, g_g, hg_lb[l], hg_norm_g[l])
        y_ml = _mlstm_branch(m_q, m_k, m_v, m_i, m_f, m_o, ml_i_bias[l], ml_f_bias[l], ml_norm_g[l])
        gate = jax.nn.sigmoid(gates.reshape(n_b, n_s, N_BRANCH, D_MODEL))
        merged = (gate[:, :, 0] * (y_ssd @ w_br_ssd[l])
                  + gate[:, :, 1] * (y_hg @ w_br_hg[l])
                  + gate[:, :, 2] * (y_ml @ w_br_ml[l]))
        h = h + merged @ w_out[l]
        u = _rmsnorm(h, norm_ffn_g[l])
        up = _dwconv(u @ w_up[l], ffn_conv_w[l], ffn_conv_b[l])
        a_half, v_half = jnp.split(up, 2, axis=-1)
        h = h + (jax.nn.silu(a_half) * v_half) @ w_down[l]
        h = h + (p[l] @ w_ple[l]) * jax.nn.sigmoid(h @ w_ple_gate[l])
    return _rmsnorm(h, final_norm_g)
```

```python
import numpy as np
import concourse.bass as bass
import concourse.mybir as mybir
from concourse.bass_utils import run_bass_kernel_spmd

F32 = mybir.dt.float32
BF16 = mybir.dt.bfloat16
AF = mybir.ActivationFunctionType
ALU = mybir.AluOpType

D = 1024
T = 2048
DEPTH = 2
PLE = 256
DFF = 2816
NIN = 14912
EPS = 1e-6
NT = T // 128
NB = T // 512
KC = D // 128

OFF_SZ = 0
OFF_XBC = 1024
OFF_DT = 2560
OFF_GQ = 2592
OFF_GF = 3616
OFF_GI = 5664
OFF_GG = 6688
OFF_MQ = 7712
OFF_MK = 8736
OFF_MV = 9760
OFF_MI = 10784
OFF_MF = 10800
OFF_MO = 10816
OFF_GATE = 11840

NDSEM = 24
NPOOLSEM = 4


class Sched:
    ENG = ("pe", "act", "dve", "pool", "sp")

    def __init__(self, nc):
        self.nc = nc
        self.eng = dict(pe=nc.tensor, act=nc.scalar, dve=nc.vector, pool=nc.gpsimd, sp=nc.sync)
        self.sem = {e: nc.alloc_semaphore(name=f"sem_{e}") for e in self.ENG}
        self.count = {e: 0 for e in self.ENG}
        self.known = {e: {} for e in self.ENG}
        self.last_w = {}
        self.readers = {}
        self.dsem, self.dcount, self.dnext = {}, {}, {}
        for q in ("sp", "act", "pool"):
            self.dsem[q] = [nc.alloc_semaphore(name=f"dsem_{q}_{i}") for i in range(NDSEM)]
            self.dcount[q] = [0] * NDSEM
            self.dnext[q] = 0
        self.n_wait = 0
        self.n_inst = 0

    def _semval(self, clock, value):
        if isinstance(clock, tuple):
            _, q, i = clock
            return self.dsem[q][i], value * 16
        return self.sem[clock], value

    def _wait(self, e, clock, value):
        if self.known[e].get(clock, 0) >= value:
            return
        if clock == e and e == "pe":
            return
        sem, v = self._semval(clock, value)
        self.eng[e].wait_ge(sem, v)
        self.known[e][clock] = value
        self.n_wait += 1

    def _deps(self, e, reads, writes):
        need = {}
        for k in reads:
            w = self.last_w.get(k)
            if w is not None:
                need[w[0]] = max(need.get(w[0], 0), w[1])
        for k in writes:
            w = self.last_w.get(k)
            if w is not None:
                need[w[0]] = max(need.get(w[0], 0), w[1])
            for c, v in self.readers.get(k, {}).items():
                need[c] = max(need.get(c, 0), v)
        for c, v in need.items():
            self._wait(e, c, v)

    def _record(self, clock, value, reads, writes):
        for k in writes:
            self.last_w[k] = (clock, value)
            self.readers[k] = {}
        for k in reads:
            self.readers.setdefault(k, {})[clock] = value

    def op(self, e, fn, reads=(), writes=()):
        self._deps(e, reads, writes)
        ins = fn(self.eng[e])
        self.count[e] += 1
        ins.then_inc(self.sem[e], 1)
        self._record(e, self.count[e], reads, writes)
        self.n_inst += 1
        return ins

    def dma(self, q, out, in_, reads=(), writes=(), **kw):
        i = self.dnext[q]
        self.dnext[q] = (i + 1) % (NPOOLSEM if q == "pool" else NDSEM)
        clock = ("d", q, i)
        if self.dcount[q][i] > 0:
            self._wait(q, clock, self.dcount[q][i])
        self._deps(q, reads, writes)
        ins = self.eng[q].dma_start(out=out, in_=in_, **kw)
        self.dcount[q][i] += 1
        ins.then_inc(self.dsem[q][i], 16)
        self._record(clock, self.dcount[q][i], reads, writes)
        self.n_inst += 1
        return ins

    def barrier(self, engines=None):
        for e in (engines or self.ENG):
            for e2 in self.ENG:
                if e2 != e and self.count[e2] > 0:
                    self._wait(e, e2, self.count[e2])
            if self.count[e] > 0 and e != "pe":
                self._wait(e, e, self.count[e])
            for q in self.dsem:
                for i in range(NDSEM):
                    if self.dcount[q][i] > 0:
                        self._wait(e, ("d", q, i), self.dcount[q][i])
        if engines is None:
            self.last_w = {}
            self.readers = {}


class Arena:
    def __init__(self, nc, nbytes):
        self.t = nc.alloc_sbuf_tensor("arena", [128, nbytes // 4], F32)
        self.nbytes = nbytes
        self.top = 0

    def at(self, off, shape, dtype):
        n = int(np.prod(shape))
        esz = 4 if dtype == F32 else 2
        nb = n * esz
        assert off % 4 == 0 and off + nb <= self.nbytes, (off, nb, self.nbytes)
        ap = self.t[:, off // 4:(off + ((nb + 3) // 4) * 4) // 4]
        if dtype != F32:
            ap = ap.bitcast(dtype)
            if n * esz % 4 != 0:
                ap = ap[:, 0:n]
        if len(shape) == 2:
            ap = ap.rearrange("p (a b) -> p a b", a=shape[0])
        elif len(shape) == 3:
            ap = ap.rearrange("p (a b c) -> p a b c", a=shape[0], b=shape[1])
        return ap

    def alloc(self, shape, dtype):
        esz = 4 if dtype == F32 else 2
        nb = ((int(np.prod(shape)) * esz + 31) // 32) * 32
        off = self.top
        self.top += nb
        assert self.top <= self.nbytes, ("arena overflow", self.top, self.nbytes)
        return self.at(off, shape, dtype)


class MK:
    def __init__(self, nseq, layers=(0, 1), dbg=None):
        self.nseq = nseq
        self.layers = layers
        self.dbg = dbg or {}
        nc = self.nc = bass.Bass("TRN2", target_bir_lowering=False)
        self.S = Sched(nc)
        dt = nc.dram_tensor
        self.inp = {}
        shapes = dict(
            x=[nseq, T, D], p=[DEPTH, nseq, T, PLE], norm_mix_g=[DEPTH, D], w_in=[DEPTH, D, NIN],
            ssd_conv_w=[DEPTH, 5, 1536], ssd_conv_b=[DEPTH, 1536], ssd_dt_bias=[DEPTH, 2, 16],
            ssd_a_log=[DEPTH, 2, 16], ssd_d=[DEPTH, 16], ssd_norm_g=[DEPTH, D], hg_lb_raw=[DEPTH, 2, D],
            hg_norm_g=[DEPTH, D], ml_i_bias=[DEPTH, 2, 8], ml_f_bias=[DEPTH, 2, 8], ml_norm_g=[DEPTH, D],
            w_br_ssd=[DEPTH, D, D], w_br_hg=[DEPTH, D, D], w_br_ml=[DEPTH, D, D], w_out=[DEPTH, D, D],
            norm_ffn_g=[DEPTH, D], w_up=[DEPTH, D, 2 * DFF], ffn_conv_w=[DEPTH, 3, 2 * DFF],
            ffn_conv_b=[DEPTH, 2 * DFF], w_down=[DEPTH, DFF, D], w_ple=[DEPTH, PLE, D],
            w_ple_gate=[DEPTH, D, D], final_norm_g=[D], c_ident=[128, 128], c_negmask=[2, 128, 128],
            c_sgn16=[16, 2], c_sgn32=[32, 2], c_rm64=[1, T], c_mask01=[2, 64, 64],
        )
        for k, s in shapes.items():
            self.inp[k] = dt(k, s, F32, kind="ExternalInput").ap()
        self.out = dt("out", [nseq, T, D], F32, kind="ExternalOutput").ap()
        self.hA = dt("hA", [nseq, T, D], F32, kind="Internal").ap()
        self.hC = dt("hC", [nseq, T, D], F32, kind="Internal").ap()
        self.yT = dt("yT", [nseq, 3, D, T], BF16, kind="Internal").ap()
        if "yT_in" in self.dbg:
            self.yT_in = dt("yT_in", [3, D, T], F32, kind="ExternalInput").ap()
        self.taps = {}
        for name, shp in self.dbg.get("taps", {}).items():
            self.taps[name] = dt("tap_" + name, shp, F32, kind="ExternalOutput").ap()
        self.A = Arena(nc, 207 * 1024)
        self.ps = [nc.alloc_psum_tensor(f"ps{i}", [128, 512], F32) for i in range(8)]
        self.psi = 0
        self.build()

    def bank(self):
        i = self.psi
        self.psi = (self.psi + 1) % 8
        return i

    def mm(self, bank, out_ap, lhsT, rhs, start, stop, reads):
        self.S.op("pe", lambda e: e.matmul(out_ap, lhsT=lhsT, rhs=rhs, start=start, stop=stop),
                  reads=reads, writes=[f"ps{bank}"])

    def tr(self, bank, out_ap, in_ap, ident, reads):
        self.S.op("pe", lambda e: e.transpose(out_ap, in_ap, ident), reads=list(reads) + ["ident"],
                  writes=[f"ps{bank}"])

    def load_w(self, dst, src, key, q="pool"):
        self.S.dma(q, dst, src.rearrange("(kc kp) c -> kp kc c", kp=128), writes=[key])

    def build(self):
        S, A, I = self.S, self.A, self.inp
        self.ident_bf = A.alloc([128], BF16)
        self.ident_f = A.alloc([128], F32)
        S.dma("pool", self.ident_bf, I["c_ident"], writes=["ident"])
        S.dma("sp", self.ident_f, I["c_ident"], writes=["ident"])
        self.base = A.top
        self.uT = A.alloc([KC, T], BF16)
        self.u2T = A.alloc([KC, T], BF16)
        self.work = A.top
        for s in range(self.nseq):
            for l in self.layers:
                self.seq_layer(s, l)
        S.barrier(engines=("sp",))

    def seq_layer(self, s, l):
        S, I = self.S, self.inp
        last = (l == DEPTH - 1)
        h_in = I["x"][s] if l == 0 else self.hC[s]
        S.barrier()
        with self.nc.named_scope("A_norm"):
            self.phase_norm(h_in, I["norm_mix_g"][l:l + 1, :], self.uT, "uT")
        if "uT" in self.taps and l == 0 and s == 0:
            S.barrier()
            tmp = self.A.at(self.work, [KC, T], F32)
            S.op("dve", lambda e: e.tensor_copy(out=tmp, in_=self.uT), reads=["uT"], writes=["tmp"])
            S.dma("sp", self.taps["uT"].rearrange("(kc kp) t -> kp kc t", kp=128), tmp, reads=["tmp"], writes=["tapo"])
        if "yT_in" in self.dbg:
            S.barrier()
            for br in range(3):
                for kc in range(KC):
                    tmp = self.A.at(self.work + (kc % 2) * T * 2, [T], BF16)
                    S.dma("pool", tmp, self.yT_in[br, kc * 128:(kc + 1) * 128, :], writes=[f"ytmp{kc % 2}"])
                    S.dma("sp", self.yT[s, br, kc * 128:(kc + 1) * 128, :], tmp, reads=[f"ytmp{kc % 2}"],
                          writes=[("yT", br, kc)])
        else:
            mix = self.dbg.get("mixers", ("ssd", "hg", "ml"))
            if "ssd" in mix:
                with self.nc.named_scope("B_ssd"):
                    self.phase_ssd(s, l)
            if "hg" in mix:
                with self.nc.named_scope("C_hg"):
                    self.phase_hg(s, l)
            if "ml" in mix:
                with self.nc.named_scope("D_ml"):
                    self.phase_ml(s, l)
        if "yT" in self.taps and s == 0 and l == self.dbg.get("tap_layer", 0):
            S.barrier()
            for br in range(3):
                for kc in range(KC):
                    tmp = self.A.at(self.work + (kc % 2) * T * 2, [T], BF16)
                    S.dma("sp", tmp, self.yT[s, br, kc * 128:(kc + 1) * 128, :], writes=[f"ytmp{kc % 2}"])
                    S.dma("pool", self.taps["yT"][br, kc * 128:(kc + 1) * 128, :], tmp, reads=[f"ytmp{kc % 2}"], writes=["tapo"])
        if self.dbg.get("stop_after_mix"):
            return
        S.barrier()
        with self.nc.named_scope("E_merge"):
            self.phase_merge(s, l, h_in)
        S.barrier()
        with self.nc.named_scope("F_up"):
            self.phase_ffn_up(s, l)
        S.barrier()
        with self.nc.named_scope("G_down"):
            self.phase_ffn_down(s, l, last)

    def norm_tile(self, xt, gbc, key_x, ut, ss, sfx):
        S = self.S
        junk = self.junk
        S.op("act", lambda e: e.activation(out=junk, in_=xt, func=AF.Square, accum_out=ss[:, 0:1]),
             reads=[key_x], writes=["junk", "ss" + sfx])
        S.op("act", lambda e: e.activation(out=ss[:, 1:2], in_=ss[:, 0:1], func=AF.Sqrt, scale=1.0 / D, bias=self.eps_ap),
             reads=["ss" + sfx, "eps"], writes=["ss" + sfx])
        S.op("dve", lambda e: e.reciprocal(out=ss[:, 2:3], in_=ss[:, 1:2]), reads=["ss" + sfx], writes=["ss" + sfx])
        S.op("dve", lambda e: e.scalar_tensor_tensor(out=ut, in0=xt, scalar=ss[:, 2:3], in1=gbc, op0=ALU.mult, op1=ALU.mult),
             reads=[key_x, "ss" + sfx, "gbc"], writes=["ut" + sfx])

    def transpose_into(self, ut, key_ut, dstT, key_dst, i):
        S = self.S
        b = self.bank()
        psb = self.ps[b][:, :].bitcast(BF16)
        for kc in range(KC):
            self.tr(b, psb[:, kc * 128:(kc + 1) * 128], ut[:, kc * 128:(kc + 1) * 128], self.ident_bf, [key_ut])
        eng = "act" if i % 2 == 0 else "dve"
        src = psb.rearrange("p (k t) -> p k t", k=KC)
        dst = dstT[:, :, i * 128:(i + 1) * 128]
        if eng == "act":
            S.op("act", lambda e: e.activation(out=dst, in_=src, func=AF.Copy), reads=[f"ps{b}"], writes=[(key_dst, i)])
        else:
            S.op("dve", lambda e: e.tensor_copy(out=dst, in_=src), reads=[f"ps{b}"], writes=[(key_dst, i)])

    def setup_norm_scratch(self, off):
        A = self.A
        self.junk = A.at(off, [D], BF16); off += D * 2
        self.eps_ap = A.at(off, [1], F32); off += 32
        self.S.op("dve", lambda e: e.memset(self.eps_ap, EPS), writes=["eps"])
        return off

    def phase_norm(self, h_in, g_row, dstT, key_dst):
        S, A = self.S, self.A
        off = self.work
        gbc = A.at(off, [D], F32); off += D * 4
        off = self.setup_norm_scratch(off)
        xts = [A.at(off + j * D * 4, [D], F32) for j in range(2)]; off += 2 * D * 4
        uts = [A.at(off + j * D * 2, [D], BF16) for j in range(2)]; off += 2 * D * 2
        sss = [A.at(off + j * 32, [4], F32) for j in range(2)]; off += 64
        S.dma("sp", gbc, g_row.partition_broadcast(128), writes=["gbc"])
        for i in range(NT):
            j = i % 2
            S.dma("sp", xts[j], h_in[i * 128:(i + 1) * 128, :], writes=[f"xt{j}"])
            self.norm_tile(xts[j], gbc, f"xt{j}", uts[j], sss[j], str(j))
            self.transpose_into(uts[j], f"ut{j}", dstT, key_dst, i)

    def phase_merge(self, s, l, h_in):
        S, A, I = self.S, self.A, self.inp
        off = self.work
        gbc = A.at(off, [D], F32); off += D * 4
        off = self.setup_norm_scratch(off)
        wout = A.at(off, [KC, D], BF16); off += KC * D * 2
        yblk = [A.at(off + br * KC * 512 * 2, [KC, 512], BF16) for br in range(3)]; off += 3 * KC * 512 * 2
        mblk = A.at(off, [KC, 512], BF16); off += KC * 512 * 2
        wp = [[A.at(off + (j * 6 + q) * KC * 128 * 2, [KC, 128], BF16) for q in range(6)] for j in range(2)]
        off += 12 * KC * 128 * 2
        sg = [A.at(off + j * 512 * 4, [512], F32) for j in range(2)]; off += 2 * 512 * 4
        acc = A.at(off, [512], F32); off += 512 * 4
        xts = [A.at(off + j * D * 4, [D], F32) for j in range(2)]; off += 2 * D * 4
        uts = [A.at(off + j * D * 2, [D], BF16) for j in range(2)]; off += 2 * D * 2
        sss = [A.at(off + j * 32, [4], F32) for j in range(2)]; off += 64
        assert off <= A.nbytes, off
        S.dma("sp", gbc, I["norm_ffn_g"][l:l + 1, :].partition_broadcast(128), writes=["gbc"])
        self.load_w(wout, I["w_out"][l], "wout")
        wbr = [I["w_br_ssd"][l], I["w_br_hg"][l], I["w_br_ml"][l]]
        it = 0
        for blk in range(NB):
            t0 = blk * 512
            for br in range(3):
                S.dma("sp", yblk[br], self.yT[s, br].rearrange("(kc kp) t -> kp kc t", kp=128)[:, :, t0:t0 + 512],
                      reads=[("yT", br, kc) for kc in range(KC)], writes=[f"yblk{br}"])
            for m in range(KC):
                j = it % 2
                it += 1
                for br in range(3):
                    self.load_w(wp[j][br], wbr[br][:, m * 128:(m + 1) * 128], f"wp{j}_{br}")
                    c0 = OFF_GATE + br * D + m * 128
                    self.load_w(wp[j][3 + br], I["w_in"][l][:, c0:c0 + 128], f"wp{j}_{3 + br}")
                for br in range(3):
                    bg = self.bank()
                    for kc in range(KC):
                        self.mm(bg, self.ps[bg][:, :], wp[j][3 + br][:, kc, :], self.uT[:, kc, t0:t0 + 512], kc == 0, kc == KC - 1,
                                [f"wp{j}_{3 + br}", "uT"] + [("uT", i) for i in range(blk * 4, blk * 4 + 4)])
                    sgj = sg[br % 2]
                    S.op("act", lambda e: e.activation(out=sgj, in_=self.ps[bg][:, :], func=AF.Sigmoid),
                         reads=[f"ps{bg}"], writes=[f"sg{br % 2}"])
                    bp = self.bank()
                    for kc in range(KC):
                        self.mm(bp, self.ps[bp][:, :], wp[j][br][:, kc, :], yblk[br][:, kc, :], kc == 0, kc == KC - 1,
                                [f"wp{j}_{br}", f"yblk{br}"])
                    if br == 0:
                        S.op("dve", lambda e: e.tensor_tensor(out=acc, in0=self.ps[bp][:, :], in1=sgj, op=ALU.mult),
                             reads=[f"ps{bp}", f"sg{br % 2}"], writes=["acc"])
                    else:
                        S.op("dve", lambda e: e.tensor_tensor(out=sgj, in0=self.ps[bp][:, :], in1=sgj, op=ALU.mult),
                             reads=[f"ps{bp}", f"sg{br % 2}"], writes=[f"sg{br % 2}"])
                        dst = mblk[:, m, :] if br == 2 else acc
                        S.op("dve", lambda e: e.tensor_tensor(out=dst, in0=acc, in1=sgj, op=ALU.add),
                             reads=["acc", f"sg{br % 2}"], writes=["mblk" if br == 2 else "acc"])
            for ti in range(4):
                i = blk * 4 + ti
                j2 = i % 2
                S.dma("sp", xts[j2], h_in[i * 128:(i + 1) * 128, :], writes=[f"xt{j2}"])
                for half in range(2):
                    b = self.bank()
                    for kc in range(KC):
                        self.mm(b, self.ps[b][:, :], mblk[:, kc, ti * 128:(ti + 1) * 128], wout[:, kc, half * 512:(half + 1) * 512],
                                kc == 0, kc == KC - 1, ["mblk", "wout"])
                    xh = xts[j2][:, half * 512:(half + 1) * 512]
                    S.op("dve", lambda e: e.tensor_tensor(out=xh, in0=self.ps[b][:, :], in1=xh, op=ALU.add),
                         reads=[f"ps{b}", f"xt{j2}"], writes=[f"xt{j2}"])
                S.dma("sp", self.hA[s, i * 128:(i + 1) * 128, :], xts[j2], reads=[f"xt{j2}"], writes=[("hA", i)])
                self.norm_tile(xts[j2], gbc, f"xt{j2}", uts[j2], sss[j2], str(j2))
                self.transpose_into(uts[j2], f"ut{j2}", self.u2T, "u2T", i)
        if "h1" in self.taps and s == 0 and l == 0:
            S.barrier()
            S.dma("sp", self.taps["h1"], self.hA[s], writes=["tapo"])

    def phase_ffn_up(self, s, l):
        S, A, I = self.S, self.A, self.inp
        off = self.work
        NJ = DFF // 128
        self.gT = A.at(off, [NJ, T], BF16); off += NJ * T * 2
        wp = [A.at(off + j * KC * 128 * 2, [KC, 128], BF16) for j in range(4)]; off += 4 * KC * 128 * 2
        up = [A.at(off + j * (T + 2) * 4, [T + 2], F32) for j in range(2)]; off += 2 * (T + 2) * 4 + 16
        cv = [A.at(off + j * T * 4, [T], F32) for j in range(2)]; off += 2 * T * 4
        cwb = A.at(off, [2 * NJ, 4], F32); off += 2 * NJ * 16
        assert off <= A.nbytes, off
        cw = I["ffn_conv_w"][l]
        with self.nc.allow_non_contiguous_dma(reason="tiny conv weight load"):
            for jj in range(3):
                S.dma("sp", cwb[:, :, jj], cw[jj].rearrange("(c p) -> p c", p=128), writes=["cwb"])
            S.dma("sp", cwb[:, :, 3], I["ffn_conv_b"][l].rearrange("(c p) -> p c", p=128), writes=["cwb"])
        for j in range(2):
            S.op("dve", lambda e: e.memset(up[j][:, 0:1], 0.0), writes=[f"up{j}"])
            S.op("dve", lambda e: e.memset(up[j][:, T + 1:T + 2], 0.0), writes=[f"up{j}"])
        it = 0
        for jc in range(NJ):
            for half in range(2):
                c = half * NJ + jc
                w = wp[it % 4]
                wk = f"wp{it % 4}"
                it += 1
                self.load_w(w, I["w_up"][l][:, c * 128:(c + 1) * 128], wk)
                for blk in range(NB):
                    b = self.bank()
                    for kc in range(KC):
                        self.mm(b, self.ps[b][:, :], w[:, kc, :], self.u2T[:, kc, blk * 512:(blk + 1) * 512], kc == 0, kc == KC - 1,
                                [wk] + [("u2T", i) for i in range(blk * 4, blk * 4 + 4)])
                    dst = up[half][:, 1 + blk * 512:1 + (blk + 1) * 512]
                    S.op("act", lambda e: e.activation(out=dst, in_=self.ps[b][:, :], func=AF.Copy),
                         reads=[f"ps{b}"], writes=[f"up{half}"])
                u = up[half]
                o = cv[half]
                S.op("dve", lambda e: e.tensor_scalar(out=o, in0=u[:, 0:T], scalar1=cwb[:, c, 0:1], scalar2=cwb[:, c, 3:4],
                                                     op0=ALU.mult, op1=ALU.add),
                     reads=[f"up{half}", "cwb"], writes=[f"cv{half}"])
                S.op("dve", lambda e: e.scalar_tensor_tensor(out=o, in0=u[:, 1:T + 1], scalar=cwb[:, c, 1:2], in1=o,
                                                            op0=ALU.mult, op1=ALU.add),
                     reads=[f"up{half}", "cwb", f"cv{half}"], writes=[f"cv{half}"])
                S.op("dve", lambda e: e.scalar_tensor_tensor(out=o, in0=u[:, 2:T + 2], scalar=cwb[:, c, 2:3], in1=o,
                                                            op0=ALU.mult, op1=ALU.add),
                     reads=[f"up{half}", "cwb", f"cv{half}"], writes=[f"cv{half}"])
            S.op("act", lambda e: e.activation(out=cv[0], in_=cv[0], func=AF.Silu), reads=["cv0"], writes=["cv0"])
            g = self.gT[:, jc, :]
            S.op("dve", lambda e: e.tensor_tensor(out=g, in0=cv[0], in1=cv[1], op=ALU.mult),
                 reads=["cv0", "cv1"], writes=[("gT", jc)])

    def phase_ffn_down(self, s, l, last):
        S, A, I = self.S, self.A, self.inp
        NJ = DFF // 128
        off = self.work + NJ * T * 2
        gbc = A.at(off, [D], F32); off += D * 4
        off = self.setup_norm_scratch(off)
        xts = [A.at(off + j * D * 4, [D], F32) for j in range(2)]; off += 2 * D * 4
        hbs = [A.at(off + j * D * 2, [D], BF16) for j in range(2)]; off += 2 * D * 2
        hT = [A.at(off + j * D * 2, [KC, 128], BF16) for j in range(2)]; off += 2 * D * 2
        pt = [A.at(off + j * PLE * 4, [PLE], F32) for j in range(2)]; off += 2 * PLE * 4
        pb = [A.at(off + j * PLE * 2, [PLE], BF16) for j in range(2)]; off += 2 * PLE * 2
        pT = [A.at(off + j * PLE * 2, [2, 128], BF16) for j in range(2)]; off += 2 * PLE * 2
        sg = [A.at(off + j * 512 * 4, [512], F32) for j in range(2)]; off += 2 * 512 * 4
        sss = [A.at(off + j * 32, [4], F32) for j in range(2)]; off += 64
        wple = A.at(off, [2, D], BF16); off += 2 * D * 2
        assert off <= A.nbytes, off
        o2 = self.base
        wdown = A.at(o2, [NJ, D], BF16); o2 += NJ * D * 2
        wpg = A.at(o2, [KC, D], BF16); o2 += KC * D * 2
        assert o2 <= self.work, o2
        for q in range(0, NJ, 2):
            self.load_w(wdown[:, q:q + 2, :], I["w_down"][l][q * 128:(q + 2) * 128, :], ("wdown", q))
        self.load_w(wpg, I["w_ple_gate"][l], "wpg")
        self.load_w(wple, I["w_ple"][l], "wple")
        if last:
            S.dma("sp", gbc, I["final_norm_g"].rearrange("(o d) -> o d", o=1).partition_broadcast(128), writes=["gbc"])
        for i in range(NT):
            j = i % 2
            S.dma("sp", xts[j], self.hA[s, i * 128:(i + 1) * 128, :], reads=[("hA", i)], writes=[f"xt{j}"])
            S.dma("sp", pt[j], I["p"][l, s, i * 128:(i + 1) * 128, :], writes=[f"pt{j}"])
            for half in range(2):
                b = self.bank()
                for q in range(NJ):
                    self.mm(b, self.ps[b][:, :], self.gT[:, q, i * 128:(i + 1) * 128], wdown[:, q, half * 512:(half + 1) * 512],
                            q == 0, q == NJ - 1, [("gT", q), ("wdown", q - q % 2)])
                xh = xts[j][:, half * 512:(half + 1) * 512]
                S.op("dve", lambda e: e.tensor_tensor(out=xh, in0=self.ps[b][:, :], in1=xh, op=ALU.add),
                     reads=[f"ps{b}", f"xt{j}"], writes=[f"xt{j}"])
            S.op("act", lambda e: e.activation(out=hbs[j], in_=xts[j], func=AF.Copy), reads=[f"xt{j}"], writes=[f"hb{j}"])
            S.op("act", lambda e: e.activation(out=pb[j], in_=pt[j], func=AF.Copy), reads=[f"pt{j}"], writes=[f"pb{j}"])
            b = self.bank()
            psb = self.ps[b][:, :].bitcast(BF16)
            for kc in range(KC):
                self.tr(b, psb[:, kc * 128:(kc + 1) * 128], hbs[j][:, kc * 128:(kc + 1) * 128], self.ident_bf, [f"hb{j}"])
            S.op("dve", lambda e: e.tensor_copy(out=hT[j], in_=psb.rearrange("p (k t) -> p k t", k=KC)),
                 reads=[f"ps{b}"], writes=[f"hT{j}"])
            b = self.bank()
            psb2 = self.ps[b][:, :].bitcast(BF16)
            for kc in range(2):
                self.tr(b, psb2[:, kc * 128:(kc + 1) * 128], pb[j][:, kc * 128:(kc + 1) * 128], self.ident_bf, [f"pb{j}"])
            S.op("dve", lambda e: e.tensor_copy(out=pT[j], in_=psb2[:, 0:256].rearrange("p (k t) -> p k t", k=2)),
                 reads=[f"ps{b}"], writes=[f"pT{j}"])
            for half in range(2):
                cs = slice(half * 512, (half + 1) * 512)
                bg = self.bank()
                for kc in range(KC):
                    self.mm(bg, self.ps[bg][:, :], hT[j][:, kc, :], wpg[:, kc, cs], kc == 0, kc == KC - 1, [f"hT{j}", "wpg"])
                sgh = sg[half]
                S.op("act", lambda e: e.activation(out=sgh, in_=self.ps[bg][:, :], func=AF.Sigmoid),
                     reads=[f"ps{bg}"], writes=[f"sg{half}"])
                bp = self.bank()
                for kc in range(2):
                    self.mm(bp, self.ps[bp][:, :], pT[j][:, kc, :], wple[:, kc, cs], kc == 0, kc == 1, [f"pT{j}", "wple"])
                S.op("dve", lambda e: e.tensor_tensor(out=sgh, in0=self.ps[bp][:, :], in1=sgh, op=ALU.mult),
                     reads=[f"ps{bp}", f"sg{half}"], writes=[f"sg{half}"])
                xh = xts[j][:, cs]
                S.op("dve", lambda e: e.tensor_tensor(out=xh, in0=xh, in1=sgh, op=ALU.add),
                     reads=[f"xt{j}", f"sg{half}"], writes=[f"xt{j}"])
            if not last:
                S.dma("sp", self.hC[s, i * 128:(i + 1) * 128, :], xts[j], reads=[f"xt{j}"], writes=[("hC", i)])
            else:
                ss = sss[j]
                S.op("act", lambda e: e.activation(out=self.junk, in_=xts[j], func=AF.Square, accum_out=ss[:, 0:1]),
                     reads=[f"xt{j}"], writes=["junk", f"ss{j}"])
                S.op("act", lambda e: e.activation(out=ss[:, 1:2], in_=ss[:, 0:1], func=AF.Sqrt, scale=1.0 / D, bias=self.eps_ap),
                     reads=[f"ss{j}", "eps"], writes=[f"ss{j}"])
                S.op("dve", lambda e: e.reciprocal(out=ss[:, 2:3], in_=ss[:, 1:2]), reads=[f"ss{j}"], writes=[f"ss{j}"])
                S.op("dve", lambda e: e.scalar_tensor_tensor(out=xts[j], in0=xts[j], scalar=ss[:, 2:3], in1=gbc,
                                                            op0=ALU.mult, op1=ALU.mult),
                     reads=[f"xt{j}", f"ss{j}", "gbc"], writes=[f"xt{j}"])
                S.dma("sp", self.out[s, i * 128:(i + 1) * 128, :], xts[j], reads=[f"xt{j}"], writes=[("out", s, i)])
        if "h3" in self.taps and s == 0 and l == 0 and not last:
            S.barrier()
            S.dma("sp", self.taps["h3"], self.hC[s], writes=["tapo"])


    def phase_ssd(self, s, l):
        S, A, I = self.S, self.A, self.inp
        S.barrier()
        NR, NH = 32, 16
        off = self.base + KC * T * 2
        xtok = A.at(off, [NT, D], BF16); off += NT * D * 2
        BT = A.at(off, [2, T], BF16); off += 2 * T * 2
        CT = A.at(off, [2, T], BF16); off += 2 * T * 2
        Btok = A.at(off, [NT, 256], BF16); off += NT * 256 * 2
        cw5 = A.at(off, [12, 8], F32); off += 12 * 8 * 4
        colp = A.at(off, [8], F32)[:NR]; off += 32
        Dbc = A.at(off, [16], F32); off += 64
        Dexp = A.at(off, [512], F32); off += 2048
        gbc = A.at(off, [D], F32); off += D * 4
        self.one_ap = A.at(off, [1], F32); off += 32
        epsl = A.at(off, [1], F32); off += 32
        wg = A.at(off, [KC, 32], BF16); off += KC * 32 * 2
        wq = [A.at(off + j * KC * 128 * 2, [KC, 128], BF16) for j in range(2)]; off += 2 * KC * 128 * 2
        zs = [A.at(off + j * 2048, [512], F32) for j in range(2)]; off += 4096
        yz = [A.at(off + j * 2048, [512], F32) for j in range(2)]; off += 4096
        yb = [A.at(off + j * 1024, [512], BF16) for j in range(2)]; off += 2048
        junk = A.at(off, [512], BF16); off += 1024
        sm = [A.at(off + j * 32, [8], F32) for j in range(2)]; off += 64
        eoff = off
        la = A.at(off, [T], F32)[:NR]; off += T * 4
        mul = A.at(off, [T], F32)[:NR]; off += T * 4
        toff = off; off += 512 + T * 4 + NT * NR * 4
        up = [A.at(off + j * (T + 4) * 4, [T + 4], F32) for j in range(2)]; off += 2 * (T + 4) * 4
        cv = [A.at(off + j * T * 4, [T], F32) for j in range(2)]; off += 2 * T * 4
        xbf = [A.at(off + j * T * 2, [T], BF16) for j in range(2)]; off += 2 * T * 2
        early_end = off
        lo = eoff
        Yg = A.at(lo, [NT, 512], F32); lo += NT * 512 * 4
        wz = A.at(lo, [KC, 512], BF16); lo += KC * 512 * 2
        yTg = A.at(lo, [4, T], BF16); lo += 4 * T * 2
        off = max(early_end, lo)
        S.op("dve", lambda e: e.memset(self.one_ap, 1.0), writes=["one"])
        S.op("dve", lambda e: e.memset(epsl, EPS), writes=["epsl"])
        with self.nc.allow_non_contiguous_dma(reason="tiny param loads"):
            S.dma("sp", colp[:, 0:1], I["ssd_dt_bias"][l].rearrange("d (h o) -> (d h) o", o=1), writes=["colp"])
            S.dma("sp", colp[:, 1:2], I["ssd_a_log"][l].rearrange("d (h o) -> (d h) o", o=1), writes=["colp"])
            for jj in range(5):
                S.dma("sp", cw5[:, :, jj], I["ssd_conv_w"][l, jj].rearrange("(c p) -> p c", p=128), writes=["cw5"])
            S.dma("sp", cw5[:, :, 5], I["ssd_conv_b"][l].rearrange("(c p) -> p c", p=128), writes=["cw5"])
        S.dma("sp", Dbc, I["ssd_d"][l:l + 1, :].partition_broadcast(128), writes=["Dbc"])
        S.dma("sp", gbc, I["ssd_norm_g"][l:l + 1, :].partition_broadcast(128), writes=["gbcs"])
        S.op("act", lambda e: e.activation(out=colp[:, 2:3], in_=colp[:, 1:2], func=AF.Exp), reads=["colp"], writes=["colp"])
        S.op("dve", lambda e: e.tensor_scalar(out=colp[:, 2:3], in0=colp[:, 2:3], scalar1=-1.0, scalar2=None, op0=ALU.mult),
             reads=["colp"], writes=["colp"])
        self.load_w(wg, I["w_in"][l][:, OFF_DT:OFF_DT + 32], "wg")

        def ev_dt(blk, b):
            sl = mul[:, blk * 512:(blk + 1) * 512]
            S.op("act", lambda e: e.activation(out=sl, in_=self.ps[b][:NR, :], func=AF.Exp, bias=colp[:, 0:1]),
                 reads=[f"ps{b}", "colp"], writes=["p_mul"])
            S.op("act", lambda e: e.activation(out=sl, in_=sl, func=AF.Ln, bias=self.one_ap[:NR, :]), reads=["p_mul", "one"], writes=["p_mul"])
            S.op("dve", lambda e: e.tensor_scalar(out=la[:, blk * 512:(blk + 1) * 512], in0=sl, scalar1=colp[:, 2:3], scalar2=None, op0=ALU.mult),
                 reads=["p_mul", "colp"], writes=["p_la"])

        self.proj_fm(wg, "wg", 32, ev_dt)
        poff = off
        tb, off = self.scalar_prep(toff, poff, la, mul, NR, NH)
        w, off = self.scalar_setup(off, 64, nstream=4)
        assert off <= A.nbytes, off
        for j in range(2):
            S.op("dve", lambda e: e.memset(up[j][:, 0:2], 0.0), writes=[f"up{j}"])
            S.op("dve", lambda e: e.memset(up[j][:, T + 2:T + 4], 0.0), writes=[f"up{j}"])
        for fc in range(12):
            j = fc % 2
            self.load_w(wq[j], I["w_in"][l][:, OFF_XBC + fc * 128:OFF_XBC + (fc + 1) * 128], f"wq{j}")

            def ev_x(blk, b, j=j):
                S.op("act", lambda e: e.activation(out=up[j][:, 2 + blk * 512:2 + (blk + 1) * 512], in_=self.ps[b][:, :], func=AF.Copy),
                     reads=[f"ps{b}"], writes=[f"up{j}"])

            self.proj_fm(wq[j], f"wq{j}", 128, ev_x)
            u, o = up[j], cv[j]
            S.op("dve", lambda e: e.tensor_scalar(out=o, in0=u[:, 0:T], scalar1=cw5[:, fc, 0:1], scalar2=cw5[:, fc, 5:6],
                                                 op0=ALU.mult, op1=ALU.add), reads=[f"up{j}", "cw5"], writes=[f"cv{j}"])
            for jj in range(1, 5):
                S.op("dve", lambda e: e.scalar_tensor_tensor(out=o, in0=u[:, jj:jj + T], scalar=cw5[:, fc, jj:jj + 1], in1=o,
                                                            op0=ALU.mult, op1=ALU.add),
                     reads=[f"up{j}", "cw5", f"cv{j}"], writes=[f"cv{j}"])
            if fc < 8:
                dstT, key = xbf[j], f"xbf{j}"
            elif fc < 10:
                dstT, key = BT[:, fc - 8, :], ("BT", fc - 8)
            else:
                dstT, key = CT[:, fc - 10, :], ("CT", fc - 10)
            S.op("act", lambda e: e.activation(out=dstT, in_=o, func=AF.Silu), reads=[f"cv{j}"], writes=[key])
            if fc < 10:
                for g8 in range(2):
                    b = self.bank()
                    psb = self.ps[b][:, :].bitcast(BF16)
                    for cc in range(8):
                        c = g8 * 8 + cc
                        self.tr(b, psb[:, cc * 128:(cc + 1) * 128], dstT[:, c * 128:(c + 1) * 128], self.ident_bf, [key])
                    if fc < 8:
                        dd, dk = xtok[:, g8 * 8:(g8 + 1) * 8, fc * 128:(fc + 1) * 128], ("xtok", fc)
                    else:
                        dd, dk = Btok[:, g8 * 8:(g8 + 1) * 8, (fc - 8) * 128:(fc - 7) * 128], ("Btok", fc - 8)
                    S.op("dve", lambda e: e.tensor_copy(out=dd, in_=psb.rearrange("p (c k) -> p c k", c=8)),
                         reads=[f"ps{b}"], writes=[dk])
        S.barrier()
        for g in range(2):
            self.load_w(wz, I["w_in"][l][:, OFF_SZ + g * 512:OFF_SZ + (g + 1) * 512], "wz")
            ykeys = [("Yg", c) for c in range(NT)]
            S.op("dve", lambda e: e.memset(Yg, 0.0), writes=ykeys)
            S.op("dve", lambda e: e.tensor_copy(out=Dexp.rearrange("p (h c) -> p h c", h=8),
                                                in_=Dbc[:, g * 8:(g + 1) * 8].unsqueeze(2).to_broadcast([128, 8, 64])),
                 reads=["Dbc"], writes=["Dexp"])
            for hp in range(4):
                gens = []
                for hq in range(2):
                    hh = hp * 2 + hq
                    h = g * 8 + hh
                    fcx = h // 2
                    xs = slice(h * 64, (h + 1) * 64)
                    ys = slice(hh * 64, (hh + 1) * 64)
                    keys = [("CT", g), ("BT", g), ("Btok", g), ("xtok", fcx)]
                    for d in range(2):
                        r = d * NH + h

                        def combine(c, bY, has_inter, qs, ys=ys):
                            yk = ("Yg", c)
                            y = Yg[:, c, ys]
                            S.op("dve", lambda e: e.tensor_tensor(out=y, in0=self.ps[bY][:, 0:64], in1=y, op=ALU.add),
                                 reads=[f"ps{bY}", yk], writes=[yk])
                            if has_inter:
                                S.op("dve", lambda e: e.scalar_tensor_tensor(out=y, in0=self.ps[bY][:, 256:320], scalar=qs, in1=y,
                                                                            op0=ALU.mult, op1=ALU.add),
                                     reads=[f"ps{bY}", "p_qs", yk], writes=[yk])

                        gens.append(self.scalar_pass(w, w["streams"][hq * 2 + d], tb, r, d, CT[:, g, :], BT[:, g, :],
                                                     Btok[:, :, g * 128:(g + 1) * 128], lambda c, xs=xs: xtok[:, c, xs], 64, keys, combine))
                self.run_streams(gens)
            for i in range(NT):
                j = i % 2
                b = self.bank()
                for kc in range(KC):
                    self.mm(b, self.ps[b][:, :], self.uT[:, kc, i * 128:(i + 1) * 128], wz[:, kc, :], kc == 0, kc == KC - 1, ["wz", ("uT", i)])
                S.op("act", lambda e: e.activation(out=zs[j], in_=self.ps[b][:, :], func=AF.Silu), reads=[f"ps{b}"], writes=[f"zs{j}"])
                S.op("dve", lambda e: e.tensor_tensor(out=yz[j], in0=xtok[:, i, g * 512:(g + 1) * 512], in1=Dexp, op=ALU.mult),
                     reads=[("xtok", fc2) for fc2 in range(g * 4, g * 4 + 4)] + ["Dexp"], writes=[f"yz{j}"])
                S.op("dve", lambda e: e.tensor_tensor(out=yz[j], in0=yz[j], in1=Yg[:, i, :], op=ALU.add),
                     reads=[("Yg", i), f"yz{j}"], writes=[f"yz{j}"])
                S.op("dve", lambda e: e.tensor_tensor(out=yz[j], in0=yz[j], in1=zs[j], op=ALU.mult),
                     reads=[f"yz{j}", f"zs{j}"], writes=[f"yz{j}"])
                smj = sm[j]
                S.op("act", lambda e: e.activation(out=junk, in_=yz[j], func=AF.Square, accum_out=smj[:, 0:1]),
                     reads=[f"yz{j}"], writes=["junks", f"sm{j}"])
                S.op("act", lambda e: e.activation(out=smj[:, 1:2], in_=smj[:, 0:1], func=AF.Sqrt, scale=1.0 / 512, bias=epsl),
                     reads=[f"sm{j}", "epsl"], writes=[f"sm{j}"])
                S.op("dve", lambda e: e.reciprocal(out=smj[:, 2:3], in_=smj[:, 1:2]), reads=[f"sm{j}"], writes=[f"sm{j}"])
                S.op("dve", lambda e: e.scalar_tensor_tensor(out=yb[j], in0=yz[j], scalar=smj[:, 2:3], in1=gbc[:, g * 512:(g + 1) * 512],
                                                            op0=ALU.mult, op1=ALU.mult),
                     reads=[f"yz{j}", f"sm{j}", "gbcs"], writes=[f"yb{j}"])
                bT = self.bank()
                psT = self.ps[bT][:, :].bitcast(BF16)
                for k in range(4):
                    self.tr(bT, psT[:, k * 128:(k + 1) * 128], yb[j][:, k * 128:(k + 1) * 128], self.ident_bf, [f"yb{j}"])
                S.op("act", lambda e: e.activation(out=yTg[:, :, i * 128:(i + 1) * 128], in_=psT[:, 0:512].rearrange("p (k t) -> p k t", k=4),
                                                   func=AF.Copy), reads=[f"ps{bT}"], writes=["yTg"])
            S.dma("sp", self.yT[s, 0, g * 512:(g + 1) * 512, :].rearrange("(k p) t -> p k t", p=128), yTg, reads=["yTg"],
                  writes=[("yT", 0, g * 4 + k) for k in range(4)])


    def phase_hg(self, s, l):
        S, A, I = self.S, self.A, self.inp
        S.barrier()
        L = 64
        NCk = T // L
        off = self.base + KC * T * 2
        rm = A.at(off, [T], F32); off += T * 4
        gbc = A.at(off, [D], F32)[:L]; off += D * 4
        m01 = A.at(off, [2, L], F32)[:L]; off += 2 * L * 4
        lbr = A.at(off, [2, 2, 8], F32); off += 128
        lbc = A.at(off, [2, 8], F32); off += 64
        oml = A.at(off, [2, 8], F32); off += 64
        epsl = A.at(off, [1], F32); off += 32
        wq = A.at(off, [KC, 128], BF16); off += KC * 128 * 2
        wf = [A.at(off + j * KC * 128 * 2, [KC, 128], BF16) for j in range(2)]; off += 2 * KC * 128 * 2
        wig = A.at(off, [KC, 256], BF16); off += KC * 256 * 2
        qT = A.at(off, [T], BF16); off += T * 2
        logf = A.at(off, [T], F32); off += T * 4
        P = A.at(off, [T], F32); off += T * 4
        t1 = A.at(off, [T], F32); off += T * 4
        t2 = A.at(off, [T], F32); off += T * 4
        sq = A.at(t1_off := off - 2 * T * 4, [NCk, 128], F32)[:L]
        kk = A.at(off, [T], BF16); off += T * 2
        sets = []
        for dd in range(2):
            sd = {}
            for nm in ("qt_", "kt_", "qh_", "kh_"):
                sd[nm] = A.at(off, [T], BF16); off += T * 2
            sd["khtok"] = A.at(off, [NCk, 128], BF16)[:L]; off += NCk * 128 * 2
            sd["edec"] = A.at(off, [NCk], F32); off += NCk * 4
            sd["st"] = A.at(off, [128], F32); off += 512
            sd["stb"] = A.at(off, [128], BF16); off += 256
            sd["Aa"] = A.at(off, [L], BF16)[:L]; off += L * 2
            sets.append(sd)
        vtok = A.at(off, [NCk, 128], BF16)[:L]; off += NCk * 128 * 2
        gstok = A.at(off, [NCk, 128], BF16)[:L]; off += NCk * 128 * 2
        O = A.at(off, [NCk, 128], F32)[:L]; off += NCk * 128 * 4
        yb = A.at(off, [NCk, 128], BF16)[:L]; off += NCk * 128 * 2
        yTh = A.at(off, [T], BF16); off += T * 2
        ssq = A.at(off, [NCk], F32)[:L]; off += NCk * 4
        assert off <= A.nbytes, off
        S.op("dve", lambda e: e.memset(epsl, EPS), writes=["epsl"])
        S.dma("sp", rm, I["c_rm64"].partition_broadcast(128), writes=["rm"])
        S.dma("sp", gbc, I["hg_norm_g"][l:l + 1, :].partition_broadcast(L), writes=["gbch"])
        S.dma("sp", m01, I["c_mask01"].rearrange("d s t -> s d t"), writes=["m01"])
        if l == 0:
            S.op("dve", lambda e: e.memset(lbc, 0.0), writes=["lbc"])
        else:
            with self.nc.allow_non_contiguous_dma(reason="tiny param load"):
                for ll in range(2):
                    for d in range(2):
                        S.dma("sp", lbr[:, ll, d, :], I["hg_lb_raw"][ll, d].rearrange("(h p) -> p h", p=128), writes=["lbr"])
            S.op("dve", lambda e: e.tensor_tensor(out=lbc, in0=lbr[:, 1], in1=lbr[:, 0], op=ALU.subtract), reads=["lbr"], writes=["lbc"])
            S.op("act", lambda e: e.activation(out=lbc, in_=lbc, func=AF.Sigmoid), reads=["lbc"], writes=["lbc"])
        S.op("dve", lambda e: e.tensor_scalar(out=oml, in0=lbc, scalar1=-1.0, scalar2=1.0, op0=ALU.mult, op1=ALU.add),
             reads=["lbc"], writes=["oml"])
        P3 = P.rearrange("p (c l) -> p c l", l=L)
        lf3 = logf.rearrange("p (c l) -> p c l", l=L)
        t13 = t1.rearrange("p (c l) -> p c l", l=L)
        for h in range(8):
            self.load_w(wq, I["w_in"][l][:, OFF_GQ + h * 128:OFF_GQ + (h + 1) * 128], "wq")
            for d in range(2):
                c0 = OFF_GF + d * D + h * 128
                self.load_w(wf[d], I["w_in"][l][:, c0:c0 + 128], f"wf{d}")
            self.load_w(wig[:, :, 0:128], I["w_in"][l][:, OFF_GI + h * 128:OFF_GI + (h + 1) * 128], "wig")
            self.load_w(wig[:, :, 128:256], I["w_in"][l][:, OFF_GG + h * 128:OFF_GG + (h + 1) * 128], "wig")

            def ev_q(blk, b):
                S.op("act", lambda e: e.activation(out=qT[:, blk * 512:(blk + 1) * 512], in_=self.ps[b][:, :], func=AF.Silu),
                     reads=[f"ps{b}"], writes=["qT"])

            self.proj_fm(wq, "wq", 128, ev_q)
            for c in range(NCk):
                b = self.bank()
                for kc in range(KC):
                    self.mm(b, self.ps[b][:L, 0:256], self.uT[:, kc, c * L:(c + 1) * L], wig[:, kc, :], kc == 0, kc == KC - 1,
                            ["wig", ("uT", c // 2)])
                S.op("act", lambda e: e.activation(out=vtok[:, c, :], in_=self.ps[b][:L, 0:128], func=AF.Copy),
                     reads=[f"ps{b}"], writes=["vtok"])
                S.op("act", lambda e: e.activation(out=gstok[:, c, :], in_=self.ps[b][:L, 128:256], func=AF.Silu),
                     reads=[f"ps{b}"], writes=["gstok"])
            okeys = [("O", c) for c in range(NCk)]
            S.op("dve", lambda e: e.memset(O, 0.0), writes=okeys)
            for d in range(2):
                sd = sets[d]
                qt_, kt_, qh_, kh_, khtok, edec = sd["qt_"], sd["kt_"], sd["qh_"], sd["kh_"], sd["khtok"], sd["edec"]
                def ev_f(blk, b, d=d):
                    sl = logf[:, blk * 512:(blk + 1) * 512]
                    S.op("act", lambda e: e.activation(out=sl, in_=self.ps[b][:, :], func=AF.Sigmoid), reads=[f"ps{b}"], writes=["logf"])
                    S.op("dve", lambda e: e.tensor_scalar(out=sl, in0=sl, scalar1=oml[:, d, h:h + 1], scalar2=lbc[:, d, h:h + 1],
                                                         op0=ALU.mult, op1=ALU.add), reads=["logf", "oml", "lbc"], writes=["logf"])
                    S.op("dve", lambda e: e.tensor_scalar(out=kk[:, blk * 512:(blk + 1) * 512], in0=sl, scalar1=-1.0, scalar2=1.0,
                                                         op0=ALU.mult, op1=ALU.add), reads=["logf"], writes=["kk"])
                    S.op("dve", lambda e: e.tensor_scalar(out=sl, in0=sl, scalar1=1e-30, scalar2=None, op0=ALU.max),
                         reads=["logf"], writes=["logf"])
                    S.op("act", lambda e: e.activation(out=sl, in_=sl, func=AF.Ln), reads=["logf"], writes=["logf"])

                self.proj_fm(wf[d], f"wf{d}", 128, ev_f)
                S.op("dve", lambda e: e.tensor_tensor_scan(out=P, data0=rm, data1=logf, initial=0.0, op0=ALU.mult, op1=ALU.add),
                     reads=["rm", "logf"], writes=["P"])
                S.op("act", lambda e: e.activation(out=edec.rearrange("p (c o) -> p c o", o=1), in_=P3[:, :, L - 1:L], func=AF.Exp),
                     reads=["P"], writes=[f"edec{d}"])
                if d == 0:
                    X, X3, xk = P, P3, "P"
                else:
                    S.op("dve", lambda e: e.tensor_tensor(out=logf, in0=P, in1=logf, op=ALU.subtract), reads=["P", "logf"], writes=["logf"])
                    X, X3, xk = logf, lf3, "logf"
                sg = 1.0 if d == 0 else -1.0
                S.op("dve", lambda e: e.tensor_tensor(out=t13, in0=X3, in1=X3[:, :, 31:32].to_broadcast([128, NCk, L]), op=ALU.subtract),
                     reads=[xk], writes=["t1"])
                S.op("act", lambda e: e.activation(out=t2, in_=t1, func=AF.Exp, scale=sg), reads=["t1"], writes=["t2"])
                S.op("dve", lambda e: e.tensor_tensor(out=qt_, in0=qT, in1=t2, op=ALU.mult), reads=["qT", "t2"], writes=[f"qt_{d}"])
                S.op("act", lambda e: e.activation(out=t2, in_=t1, func=AF.Exp, scale=-sg), reads=["t1", f"qt_{d}"], writes=["t2"])
                S.op("dve", lambda e: e.tensor_tensor(out=kt_, in0=kk, in1=t2, op=ALU.mult), reads=["kk", "t2"], writes=[f"kt_{d}"])
                S.op("act", lambda e: e.activation(out=t2, in_=X, func=AF.Exp), reads=[xk, f"kt_{d}"], writes=["t2"])
                e1q = qh_ if d == 0 else kh_
                S.op("dve", lambda e: e.tensor_tensor(out=e1q, in0=(qT if d == 0 else kk), in1=t2, op=ALU.mult),
                     reads=["qT", "kk", "t2"], writes=[f"qh_{d}" if d == 0 else f"kh_{d}"])
                S.op("dve", lambda e: e.tensor_tensor(out=t13, in0=P3[:, :, L - 1:L].to_broadcast([128, NCk, L]), in1=X3, op=ALU.subtract),
                     reads=["P", xk, f"qt_{d}", f"kt_{d}"], writes=["t1"])
                S.op("act", lambda e: e.activation(out=t2, in_=t1, func=AF.Exp), reads=["t1", f"qh_{d}", f"kh_{d}"], writes=["t2"])
                e2q = kh_ if d == 0 else qh_
                S.op("dve", lambda e: e.tensor_tensor(out=e2q, in0=(kk if d == 0 else qT), in1=t2, op=ALU.mult),
                     reads=["qT", "kk", "t2"], writes=[f"kh_{d}" if d == 0 else f"qh_{d}"])
                for g8 in range(NCk // 8):
                    b = self.bank()
                    psb = self.ps[b][:, :].bitcast(BF16)
                    for cc in range(8):
                        c = g8 * 8 + cc
                        self.tr(b, psb[:L, cc * 128:(cc + 1) * 128], kh_[:, c * L:(c + 1) * L], self.ident_bf, [f"kh_{d}"])
                    S.op("act", lambda e: e.activation(out=khtok[:, g8 * 8:(g8 + 1) * 8, :], in_=psb[:L, :].rearrange("p (c k) -> p c k", c=8),
                                                       func=AF.Copy), reads=[f"ps{b}"], writes=[f"khtok{d}"])
            def hg_pass(d):
                sd = sets[d]
                qt_, kt_, qh_, kh_, khtok, edec, st, stb, Aa = (sd[k] for k in ("qt_", "kt_", "qh_", "kh_", "khtok", "edec", "st", "stb", "Aa"))
                order = range(NCk) if d == 0 else range(NCk - 1, -1, -1)
                first = True
                for c in order:
                    cs = slice(c * L, (c + 1) * L)
                    bA = self.bank()
                    self.mm(bA, self.ps[bA][:L, 0:L], kt_[:, cs], qt_[:, cs], True, True, [f"kt_{d}", f"qt_{d}"])
                    yield
                    S.op("dve", lambda e: e.tensor_tensor(out=Aa, in0=self.ps[bA][:L, 0:L], in1=m01[:, d, :], op=ALU.mult),
                         reads=[f"ps{bA}", "m01"], writes=[f"Aa{d}"])
                    yield
                    bO = self.bank()
                    self.mm(bO, self.ps[bO][:L, 0:128], Aa, vtok[:, c, :], True, first, [f"Aa{d}", "vtok"])
                    if not first:
                        self.mm(bO, self.ps[bO][:L, 0:128], qh_[:, cs], stb, False, True, [f"qh_{d}", f"stb{d}"])
                    bE = self.bank()
                    self.mm(bE, self.ps[bE][:, 0:128], khtok[:, c, :], vtok[:, c, :], True, True, [f"khtok{d}", "vtok"])
                    yield
                    S.op("dve", lambda e: e.tensor_tensor(out=O[:, c, :], in0=self.ps[bO][:L, 0:128], in1=O[:, c, :], op=ALU.add),
                         reads=[f"ps{bO}", ("O", c)], writes=[("O", c)])
                    if first:
                        S.op("dve", lambda e: e.tensor_copy(out=st, in_=self.ps[bE][:, 0:128]), reads=[f"ps{bE}"], writes=[f"st{d}"])
                    else:
                        S.op("dve", lambda e: e.scalar_tensor_tensor(out=st, in0=st, scalar=edec[:, c:c + 1], in1=self.ps[bE][:, 0:128],
                                                                    op0=ALU.mult, op1=ALU.add),
                             reads=[f"st{d}", f"edec{d}", f"ps{bE}"], writes=[f"st{d}"])
                    S.op("act", lambda e: e.activation(out=stb, in_=st, func=AF.Copy), reads=[f"st{d}"], writes=[f"stb{d}"])
                    first = False
                    yield

            self.run_streams([hg_pass(0), hg_pass(1)])
            S.op("dve", lambda e: e.tensor_tensor(out=sq, in0=O, in1=O, op=ALU.mult), reads=okeys + ["t1", "t2"], writes=["t1", "t2"])
            S.op("dve", lambda e: e.tensor_reduce(out=ssq, in_=sq, axis=mybir.AxisListType.X, op=ALU.add), reads=["t1", "t2"], writes=["ssq"])
            S.op("act", lambda e: e.activation(out=ssq, in_=ssq, func=AF.Sqrt, scale=1.0 / 128, bias=epsl[:L, :]), reads=["ssq", "epsl"], writes=["ssq"])
            S.op("dve", lambda e: e.reciprocal(out=ssq, in_=ssq), reads=["ssq"], writes=["ssq"])
            S.op("dve", lambda e: e.tensor_tensor(out=O, in0=O, in1=ssq.unsqueeze(2).to_broadcast([L, NCk, 128]), op=ALU.mult),
                 reads=okeys + ["ssq"], writes=okeys)
            S.op("dve", lambda e: e.tensor_tensor(out=O, in0=O, in1=gbc[:, h * 128:(h + 1) * 128].unsqueeze(1).to_broadcast([L, NCk, 128]), op=ALU.mult),
                 reads=okeys + ["gbch"], writes=okeys)
            S.op("dve", lambda e: e.tensor_tensor(out=yb, in0=O, in1=gstok, op=ALU.mult), reads=okeys + ["gstok"], writes=["ybh"])
            for g16 in range(NCk // 16):
                b = self.bank()
                psb = self.ps[b][:, :].bitcast(BF16)
                for cc in range(16):
                    c = g16 * 16 + cc
                    self.tr(b, psb[:, cc * L:(cc + 1) * L], yb[:, c, :], self.ident_bf[:L, :L], ["ybh"])
                S.op("act", lambda e: e.activation(out=yTh[:, g16 * 1024:(g16 + 1) * 1024], in_=psb, func=AF.Copy),
                     reads=[f"ps{b}"], writes=["yThh"])
            S.dma("sp", self.yT[s, 1, h * 128:(h + 1) * 128, :], yTh, reads=["yThh"], writes=[("yT", 1, h)])


    def scalar_prep(self, toff, off, la, mul, NR, NH):
        S, A = self.S, self.A
        W = NT * NR
        ones = A.at(toff, [128], F32); toff += 512
        acum = A.at(toff, [T], F32)[:NR]; toff += T * 4
        bd = A.at(toff, [NT, NR], F32)[:NR]; toff += W * 4
        tb = {}
        for nm in ("tokM", "nb", "multok", "aend", "eend", "qs", "ws"):
            tb[nm] = A.at(off, [NT, NR], F32); off += W * 4
        S.op("dve", lambda e: e.memset(ones, 1.0), writes=["p_ones"])
        for c in range(NT):
            cs = slice(c * 128, (c + 1) * 128)
            S.op("dve", lambda e: e.tensor_tensor_scan(out=acum[:, cs], data0=ones[:NR, :], data1=la[:, cs], initial=0.0,
                                                       op0=ALU.mult, op1=ALU.add), reads=["p_la", "p_ones"], writes=["p_acum"])
        aend = acum.rearrange("r (c l) -> r c l", l=128)[:, :, 127:128]
        S.op("dve", lambda e: e.tensor_tensor(out=bd, in0=aend.to_broadcast([NR, NT, NR]),
                                              in1=self.ident_f[:NR, :NR].unsqueeze(1).to_broadcast([NR, NT, NR]), op=ALU.mult),
             reads=["p_acum", "ident"], writes=["p_bd"])
        b = self.bank()
        self.S.op("pe", lambda e: e.matmul(self.ps[b][:, 0:W], lhsT=ones[:NR, :], rhs=bd.rearrange("r c q -> r (c q)"), start=True, stop=True),
                  reads=["p_ones", "p_bd"], writes=[f"ps{b}"])
        S.op("act", lambda e: e.activation(out=tb["aend"].rearrange("p c r -> p (c r)"), in_=self.ps[b][:, 0:W], func=AF.Copy),
             reads=[f"ps{b}"], writes=["p_aend"])
        rowM = A.at(off, [T], F32)[:NR]; off += T * 4
        sgn = A.at(off, [2], F32)[:NR]; off += 32
        S.dma("sp", sgn, self.inp["c_sgn%d" % NR], writes=["p_sgn"])
        S.op("dve", lambda e: e.tensor_scalar(out=rowM, in0=la, scalar1=sgn[:, 1:2], scalar2=None, op0=ALU.mult),
             reads=["p_la", "p_sgn"], writes=["p_rowM"])
        S.op("dve", lambda e: e.scalar_tensor_tensor(out=rowM, in0=acum, scalar=sgn[:, 0:1], in1=rowM, op0=ALU.mult, op1=ALU.add),
             reads=["p_acum", "p_sgn", "p_rowM"], writes=["p_rowM"])
        for nm, src, key in (("tokM", rowM, "p_rowM"), ("multok", mul, "p_mul")):
            b = self.bank()
            for c in range(NT):
                self.S.op("pe", lambda e: e.transpose(self.ps[b][:, c * NR:(c + 1) * NR], src[:, c * 128:(c + 1) * 128], self.ident_f[:NR, :NR]),
                          reads=[key, "ident"], writes=[f"ps{b}"])
            S.op("act", lambda e: e.activation(out=tb[nm].rearrange("p c r -> p (c r)"), in_=self.ps[b][:, 0:W], func=AF.Copy),
                 reads=[f"ps{b}"], writes=["p_" + nm])
        S.op("dve", lambda e: e.tensor_scalar(out=tb["nb"], in0=tb["tokM"], scalar1=-1.0, scalar2=None, op0=ALU.mult),
             reads=["p_tokM"], writes=["p_nb"])
        S.op("act", lambda e: e.activation(out=tb["eend"], in_=tb["aend"], func=AF.Exp), reads=["p_aend"], writes=["p_eend"])
        f, bw = slice(0, NH), slice(NH, NR)
        S.op("act", lambda e: e.activation(out=tb["qs"][:, :, f], in_=tb["tokM"][:, :, f], func=AF.Exp), reads=["p_tokM"], writes=["p_qs"])
        S.op("dve", lambda e: e.tensor_tensor(out=tb["ws"][:, :, f], in0=tb["aend"][:, :, f], in1=tb["tokM"][:, :, f], op=ALU.subtract),
             reads=["p_aend", "p_tokM"], writes=["p_ws"])
        S.op("dve", lambda e: e.tensor_tensor(out=tb["qs"][:, :, bw], in0=tb["aend"][:, :, bw], in1=tb["tokM"][:, :, bw], op=ALU.add),
             reads=["p_aend", "p_tokM", "p_qs"], writes=["p_qs"])
        S.op("act", lambda e: e.activation(out=tb["qs"][:, :, bw], in_=tb["qs"][:, :, bw], func=AF.Exp), reads=["p_qs"], writes=["p_qs"])
        S.op("dve", lambda e: e.tensor_copy(out=tb["ws"][:, :, bw], in_=tb["nb"][:, :, bw]), reads=["p_nb", "p_ws"], writes=["p_ws"])
        S.op("act", lambda e: e.activation(out=tb["ws"], in_=tb["ws"], func=AF.Exp), reads=["p_ws"], writes=["p_ws"])
        S.op("dve", lambda e: e.tensor_tensor(out=tb["ws"], in0=tb["ws"], in1=tb["multok"], op=ALU.mult),
             reads=["p_ws", "p_multok"], writes=["p_ws"])
        tb["rowM"] = rowM
        tb["NR"] = NR
        return tb, off

    def scalar_setup(self, off, dvmax, nstream=4):
        A = self.A
        w = {}
        w["negmask"] = A.at(off, [2, 128], BF16); off += 2 * 128 * 2
        self.S.dma("pool", w["negmask"], self.inp["c_negmask"].rearrange("d s t -> s d t"), writes=["negmask"])
        dvp = ((dvmax + 15) // 16) * 16
        w["streams"] = []
        for i in range(nstream):
            sw = {"id": i}
            sw["MT"] = A.at(off, [128], F32); off += 512
            sw["G"] = A.at(off, [128], BF16); off += 256
            sw["vw"] = A.at(off, [dvp], BF16); off += dvp * 2
            sw["st"] = A.at(off, [dvp], F32); off += dvp * 4
            sw["stb"] = A.at(off, [dvp], BF16); off += dvp * 2
            w["streams"].append(sw)
        return w, off

    def run_streams(self, gens):
        active = list(gens)
        while active:
            for g in list(active):
                try:
                    next(g)
                except StopIteration:
                    active.remove(g)

    def scalar_pass(self, w, sw, tb, r, d, qT, kT, ktok, vtok, dv, keys, combine):
        S = self.S
        NR = tb["NR"]
        sid = sw["id"]
        sel = self.ident_f[:NR, r:r + 1].to_broadcast([NR, 128])
        st, stb = sw["st"][:, 0:dv], sw["stb"][:, 0:dv]
        MT, G, vw = sw["MT"], sw["G"], sw["vw"][:, 0:dv]
        kMT, kG, kvw, kst, kstb = f"MT{sid}", f"G{sid}", f"vw{sid}", f"st{sid}", f"stb{sid}"
        order = range(NT) if d == 0 else range(NT - 1, -1, -1)
        first = True
        for c in order:
            cs = slice(c * 128, (c + 1) * 128)
            bX = self.bank()
            self.mm(bX, self.ps[bX][:, 0:128], kT[:, cs], qT[:, cs], True, True, keys)
            S.op("pe", lambda e: e.matmul(self.ps[bX][:, 256:384], lhsT=sel, rhs=tb["rowM"][:, cs], start=True, stop=False),
                 reads=["ident", "p_rowM"], writes=[f"ps{bX}"])
            S.op("pe", lambda e: e.matmul(self.ps[bX][:, 256:384], lhsT=self.ident_bf, rhs=w["negmask"][:, d, :], start=False, stop=True),
                 reads=["ident", "negmask"], writes=[f"ps{bX}"])
            yield
            S.op("act", lambda e: e.activation(out=MT, in_=self.ps[bX][:, 256:384], func=AF.Exp, bias=tb["nb"][:, c, r:r + 1]),
                 reads=[f"ps{bX}", "p_nb"], writes=[kMT])
            yield
            S.op("dve", lambda e: e.scalar_tensor_tensor(out=G, in0=MT, scalar=tb["multok"][:, c, r:r + 1], in1=self.ps[bX][:, 0:128],
                                                        op0=ALU.mult, op1=ALU.mult),
                 reads=[kMT, "p_multok", f"ps{bX}"], writes=[kG])
            yield
            v = vtok(c)
            bY = self.bank()
            self.mm(bY, self.ps[bY][:, 0:dv], G, v, True, True, [kG] + keys)
            if not first:
                self.mm(bY, self.ps[bY][:, 256:256 + dv], qT[:, cs], stb, True, True, keys + [kstb])
            S.op("dve", lambda e: e.tensor_scalar(out=vw, in0=v, scalar1=tb["ws"][:, c, r:r + 1], scalar2=None, op0=ALU.mult),
                 reads=keys + ["p_ws"], writes=[kvw])
            yield
            combine(c, bY, not first, tb["qs"][:, c, r:r + 1])
            bZ = self.bank()
            self.mm(bZ, self.ps[bZ][:, 0:dv], ktok[:, c, :], vw, True, True, keys + [kvw])
            yield
            if first:
                S.op("dve", lambda e: e.tensor_copy(out=st, in_=self.ps[bZ][:, 0:dv]), reads=[f"ps{bZ}"], writes=[kst])
            else:
                S.op("dve", lambda e: e.scalar_tensor_tensor(out=st, in0=st, scalar=tb["eend"][:, c, r:r + 1], in1=self.ps[bZ][:, 0:dv],
                                                            op0=ALU.mult, op1=ALU.add),
                     reads=[kst, "p_eend", f"ps{bZ}"], writes=[kst])
            S.op("act", lambda e: e.activation(out=stb, in_=st, func=AF.Copy), reads=[kst], writes=[kstb])
            first = False
            yield

    def proj_fm(self, w, wkey, ncols, evac):
        for blk in range(NB):
            b = self.bank()
            for kc in range(KC):
                self.mm(b, self.ps[b][:ncols, :], w[:, kc, 0:ncols], self.uT[:, kc, blk * 512:(blk + 1) * 512], kc == 0, kc == KC - 1,
                        [wkey, "uT"] + [("uT", i) for i in range(blk * 4, blk * 4 + 4)])
            evac(blk, b)

    def phase_ml(self, s, l):
        S, A, I = self.S, self.A, self.inp
        S.barrier()
        NR, NH = 16, 8
        off = self.base + KC * T * 2
        wg = A.at(off, [KC, 32], BF16); off += KC * 32 * 2
        bia = A.at(off, [4], F32)[:NR]; off += 32
        la = A.at(off, [T], F32)[:NR]; off += T * 4
        mul = A.at(off, [T], F32)[:NR]; off += T * 4
        self.load_w(wg, I["w_in"][l][:, OFF_MI:OFF_MI + 32], "wg")
        with self.nc.allow_non_contiguous_dma(reason="tiny bias load"):
            S.dma("sp", bia[:, 0:1], I["ml_i_bias"][l].rearrange("d (h o) -> (d h) o", o=1), writes=["bia"])
            S.dma("sp", bia[:, 1:2], I["ml_f_bias"][l].rearrange("d (h o) -> (d h) o", o=1), writes=["bia"])
        S.op("dve", lambda e: e.tensor_scalar(out=bia[:, 2:3], in0=bia[:, 1:2], scalar1=-1.0, scalar2=None, op0=ALU.mult),
             reads=["bia"], writes=["bia"])

        def ev_i(blk, b):
            S.op("act", lambda e: e.activation(out=mul[:, blk * 512:(blk + 1) * 512], in_=self.ps[b][:NR, :], func=AF.Exp, bias=bia[:, 0:1]),
                 reads=[f"ps{b}", "bia"], writes=["p_mul"])

        def ev_f(blk, b):
            sl = la[:, blk * 512:(blk + 1) * 512]
            S.op("act", lambda e: e.activation(out=sl, in_=self.ps[b][:NR, :], func=AF.Exp, scale=-1.0, bias=bia[:, 2:3]),
                 reads=[f"ps{b}", "bia"], writes=["p_la"])
            S.op("act", lambda e: e.activation(out=sl, in_=sl, func=AF.Ln, bias=self.one_ap[:NR, :]), reads=["p_la", "one"], writes=["p_la"])
            S.op("dve", lambda e: e.tensor_scalar(out=sl, in0=sl, scalar1=-1.0, scalar2=None, op0=ALU.mult), reads=["p_la"], writes=["p_la"])

        self.one_ap = A.at(off, [1], F32); off += 32
        S.op("dve", lambda e: e.memset(self.one_ap, 1.0), writes=["one"])
        self.proj_fm(wg[:, :, 0:16], "wg", 16, ev_i)
        self.proj_fm(wg[:, :, 16:32], "wg", 16, ev_f)
        toff = off; off += 512 + T * 4 + NT * NR * 4
        tb, off = self.scalar_prep(toff, off, la, mul, NR, NH)
        w, off = self.scalar_setup(off, 130, nstream=2)
        wq = [A.at(off + j * KC * 128 * 2, [KC, 128], BF16) for j in range(4)]; off += 4 * KC * 128 * 2
        qT = A.at(off, [T], BF16); off += T * 2
        kT = A.at(off, [T], BF16); off += T * 2
        ktok = A.at(off, [NT, 128], BF16); off += T * 2
        vaug = A.at(off, [NT, 130], BF16); off += NT * 130 * 2
        og = A.at(off, [NT, 128], BF16); off += T * 2
        Hh = A.at(off, [NT, 128], F32); off += T * 4
        oi = [A.at(off + j * 130 * 4, [130], F32) for j in range(2)]; off += 2 * 130 * 4 + 16
        sm = [A.at(off + j * 32, [8], F32) for j in range(2)]; off += 64
        gbc = A.at(off, [D], F32); off += D * 4
        hc = [A.at(off + j * 512, [128], F32) for j in range(2)]; off += 1024
        yb = [A.at(off + j * 256, [128], BF16) for j in range(2)]; off += 512
        yTh = A.at(off, [T], BF16); off += T * 2
        junk = A.at(off, [128], F32); off += 512
        epsl = A.at(off, [1], F32); off += 32
        assert off <= A.nbytes, off
        S.op("dve", lambda e: e.memset(epsl, EPS), writes=["epsl"])
        S.dma("sp", gbc, I["ml_norm_g"][l:l + 1, :].partition_broadcast(128), writes=["gbcm"])
        S.op("dve", lambda e: e.memset(vaug[:, :, 128:129], 1.0), writes=["vaug1"])
        for h in range(NH):
            cols = [OFF_MQ + h * 128, OFF_MK + h * 128, OFF_MV + h * 128, OFF_MO + h * 128]
            for j in range(4):
                self.load_w(wq[j], I["w_in"][l][:, cols[j]:cols[j] + 128], f"wq{j}")

            def ev_q(blk, b):
                S.op("act", lambda e: e.activation(out=qT[:, blk * 512:(blk + 1) * 512], in_=self.ps[b][:, :], func=AF.Copy),
                     reads=[f"ps{b}"], writes=["qT"])

            def ev_k(blk, b):
                S.op("act", lambda e: e.activation(out=kT[:, blk * 512:(blk + 1) * 512], in_=self.ps[b][:, :], func=AF.Copy, scale=128 ** -0.5),
                     reads=[f"ps{b}"], writes=["kT"])

            self.proj_fm(wq[0], "wq0", 128, ev_q)
            self.proj_fm(wq[1], "wq1", 128, ev_k)
            for g in range(2):
                b = self.bank()
                psb = self.ps[b][:, :].bitcast(BF16)
                for cc in range(8):
                    c = g * 8 + cc
                    self.tr(b, psb[:, cc * 128:(cc + 1) * 128], kT[:, c * 128:(c + 1) * 128], self.ident_bf, ["kT"])
                S.op("dve", lambda e: e.tensor_copy(out=ktok[:, g * 8:(g + 1) * 8, :], in_=psb.rearrange("p (c k) -> p c k", c=8)),
                     reads=[f"ps{b}"], writes=["ktok"])
            for i in range(NT):
                b = self.bank()
                for q2, wj in ((0, 2), (1, 3)):
                    for kc in range(KC):
                        self.mm(b, self.ps[b][:, q2 * 128:(q2 + 1) * 128], self.uT[:, kc, i * 128:(i + 1) * 128], wq[wj][:, kc, :],
                                kc == 0, kc == KC - 1, [f"wq{wj}", ("uT", i)])
                S.op("act", lambda e: e.activation(out=vaug[:, i, 0:128], in_=self.ps[b][:, 0:128], func=AF.Copy),
                     reads=[f"ps{b}"], writes=["vaug"])
                S.op("act", lambda e: e.activation(out=og[:, i, :], in_=self.ps[b][:, 128:256], func=AF.Sigmoid),
                     reads=[f"ps{b}"], writes=["og"])
            keys = ["qT", "kT", "ktok", "vaug", "vaug1"]
            hkeys = [("Hh", c) for c in range(NT)]
            S.op("dve", lambda e: e.memset(Hh, 0.0), writes=hkeys)
            gens = []
            for d in range(2):
                r = d * NH + h

                def combine(c, bY, has_inter, qs, d=d):
                    j = d
                    o = oi[j][:, 0:129]
                    S.op("act", lambda e: e.activation(out=o, in_=self.ps[bY][:, 0:129], func=AF.Copy),
                         reads=[f"ps{bY}"], writes=[f"oi{j}"])
                    if has_inter:
                        S.op("dve", lambda e: e.scalar_tensor_tensor(out=o, in0=self.ps[bY][:, 256:385], scalar=qs, in1=o,
                                                                    op0=ALU.mult, op1=ALU.add),
                             reads=[f"ps{bY}", "p_qs", f"oi{j}"], writes=[f"oi{j}"])
                    smj = sm[j]
                    S.op("dve", lambda e: e.scalar_tensor_tensor(out=smj[:, 0:1], in0=o[:, 128:129], scalar=-1.0, in1=o[:, 128:129],
                                                                op0=ALU.mult, op1=ALU.max),
                         reads=[f"oi{j}"], writes=[f"sm{j}"])
                    S.op("dve", lambda e: e.tensor_scalar(out=smj[:, 0:1], in0=smj[:, 0:1], scalar1=1.0, scalar2=None, op0=ALU.max),
                         reads=[f"sm{j}"], writes=[f"sm{j}"])
                    S.op("dve", lambda e: e.reciprocal(out=smj[:, 1:2], in_=smj[:, 0:1]), reads=[f"sm{j}"], writes=[f"sm{j}"])
                    S.op("dve", lambda e: e.scalar_tensor_tensor(out=Hh[:, c, :], in0=o[:, 0:128], scalar=smj[:, 1:2], in1=Hh[:, c, :],
                                                                op0=ALU.mult, op1=ALU.add),
                         reads=[f"oi{j}", f"sm{j}", ("Hh", c)], writes=[("Hh", c)])

                gens.append(self.scalar_pass(w, w["streams"][d], tb, r, d, qT, kT, ktok, lambda c: vaug[:, c, 0:129], 129, keys, combine))
            self.run_streams(gens)
            for c in range(NT):
                j = c % 2
                smj = sm[j]
                S.op("dve", lambda e: e.reduce_sum(out=smj[:, 2:3], in_=Hh[:, c, :], axis=mybir.AxisListType.X),
                     reads=[("Hh", c)], writes=[f"sm{j}"])
                S.op("dve", lambda e: e.tensor_scalar(out=smj[:, 3:4], in0=smj[:, 2:3], scalar1=-1.0 / 128, scalar2=None, op0=ALU.mult),
                     reads=[f"sm{j}"], writes=[f"sm{j}"])
                S.op("dve", lambda e: e.tensor_scalar(out=hc[j], in0=Hh[:, c, :], scalar1=smj[:, 3:4], scalar2=None, op0=ALU.add),
                     reads=[("Hh", c), f"sm{j}"], writes=[f"hc{j}"])
                S.op("act", lambda e: e.activation(out=junk, in_=hc[j], func=AF.Square, accum_out=smj[:, 4:5]),
                     reads=[f"hc{j}"], writes=["junkm", f"sm{j}"])
                S.op("act", lambda e: e.activation(out=smj[:, 5:6], in_=smj[:, 4:5], func=AF.Sqrt, scale=1.0 / 128, bias=epsl),
                     reads=[f"sm{j}", "epsl"], writes=[f"sm{j}"])
                S.op("dve", lambda e: e.reciprocal(out=smj[:, 6:7], in_=smj[:, 5:6]), reads=[f"sm{j}"], writes=[f"sm{j}"])
                S.op("dve", lambda e: e.scalar_tensor_tensor(out=hc[j], in0=hc[j], scalar=smj[:, 6:7], in1=gbc[:, h * 128:(h + 1) * 128],
                                                            op0=ALU.mult, op1=ALU.mult),
                     reads=[f"hc{j}", f"sm{j}", "gbcm"], writes=[f"hc{j}"])
                S.op("dve", lambda e: e.tensor_tensor(out=yb[j], in0=hc[j], in1=og[:, c, :], op=ALU.mult),
                     reads=[f"hc{j}", "og"], writes=[f"yb{j}"])
                if c % 8 == 0:
                    bT = self.bank()
                    psT = self.ps[bT][:, :].bitcast(BF16)
                self.tr(bT, psT[:, (c % 8) * 128:(c % 8 + 1) * 128], yb[j], self.ident_bf, [f"yb{j}"])
                if c % 8 == 7:
                    g8 = c // 8
                    S.op("act", lambda e: e.activation(out=yTh[:, g8 * 1024:(g8 + 1) * 1024], in_=psT, func=AF.Copy),
                         reads=[f"ps{bT}"], writes=["yTh"])
            S.dma("sp", self.yT[s, 2, h * 128:(h + 1) * 128, :], yTh, reads=["yTh"], writes=[("yT", 2, h)])


_CACHE = {}


def _consts():
    s = np.arange(128)[:, None]
    t = np.arange(128)[None, :]
    negmask = np.stack([np.where(s <= t, 0.0, -1e30), np.where(s >= t, 0.0, -1e30)]).astype(np.float32)

    def sgn(nr):
        a = np.zeros((nr, 2), np.float32)
        a[:nr // 2, 0] = 1.0
        a[nr // 2:, 0] = -1.0
        a[nr // 2:, 1] = 1.0
        return a
    rm = np.ones((1, T), np.float32)
    rm[0, ::64] = 0.0
    s6 = np.arange(64)[:, None]
    t6 = np.arange(64)[None, :]
    m01 = np.stack([(s6 <= t6), (s6 >= t6)]).astype(np.float32)
    return {"c_ident": np.eye(128, dtype=np.float32), "c_negmask": negmask, "c_sgn16": sgn(16), "c_sgn32": sgn(32),
            "c_rm64": rm, "c_mask01": m01}


def kernel(**inputs):
    n_cores = 8
    nseq = 32 // n_cores
    if "mk" not in _CACHE:
        _CACHE["mk"] = MK(nseq)
    mk = _CACHE["mk"]
    in_maps = []
    consts = _consts()
    for c in range(n_cores):
        m = {}
        for k, v in inputs.items():
            v = np.asarray(v, dtype=np.float32)
            if k == "x":
                m[k] = np.ascontiguousarray(v[c * nseq:(c + 1) * nseq])
            elif k == "p":
                m[k] = np.ascontiguousarray(v[:, c * nseq:(c + 1) * nseq])
            else:
                m[k] = v
        m.update(consts)
        in_maps.append(m)
    res = run_bass_kernel_spmd(mk.nc, in_maps, core_ids=list(range(n_cores)))
    return np.concatenate([r["out"] for r in res.results], axis=0).astype(np.float32)
```
